# Optimizing a Trainium2 kernel written in Bass

```python
import jax, jax.numpy as jnp
from jax import lax
import numpy as np

D_MODEL = 1024
BATCH = 32
SEQ = 2048
DEPTH = 1

D_MIX = D_MODEL
D_A = D_MIX // 2
D_B = D_MIX - D_A
N_HEADS_A = 4
HEAD_DIM_A = D_A // N_HEADS_A
CHUNK = 128
N_GROUPS_B = 4
GROUP_DIM_B = D_B // N_GROUPS_B
D_IN = 3 * D_A + 2 * D_B
EPS = 1e-6

kernel_name = "hybrid_gmlp_fnet_encoder_layer"


def rms_norm(x, g):
    xf = x.astype(jnp.float32)
    ms = jnp.mean(xf * xf, axis=-1, keepdims=True)
    return (xf * lax.rsqrt(ms + EPS)).astype(x.dtype) * g


def layer_norm(x, g, b):
    xf = x.astype(jnp.float32)
    mu = jnp.mean(xf, axis=-1, keepdims=True)
    var = jnp.mean(jnp.square(xf - mu), axis=-1, keepdims=True)
    return ((xf - mu) * lax.rsqrt(var + EPS)).astype(x.dtype) * g + b


def spatial_gating(u, v, w_s, b_s, ln_g, ln_b):
    bsz, s, _ = u.shape
    v = layer_norm(v, ln_g, ln_b)
    vc = v.reshape(bsz, s // CHUNK, CHUNK, N_HEADS_A, HEAD_DIM_A)
    mixed = jnp.einsum('hpq,bcqhd->bcphd', w_s, vc) + b_s.T[None, None, :, :, None]
    return u * mixed.reshape(bsz, s, D_A)


def fourier_mix(z, w_f, b_f):
    bsz, s, _ = z.shape
    zg = z.reshape(bsz, s, N_GROUPS_B, GROUP_DIM_B).astype(jnp.float32)
    f = jnp.fft.fftn(zg, axes=(1, 3), norm="ortho").real.astype(z.dtype)
    y = jnp.einsum('bsgc,gce->bsge', f, w_f) + b_f
    return y.reshape(bsz, s, D_B)


def setup_inputs(seed: int = 0) -> dict:
    key = jax.random.key(seed)
    ks = jax.random.split(key, 16)
    f32 = jnp.float32
    x = jax.random.normal(ks[0], (BATCH, SEQ, D_MODEL), f32)
    pre_g = 1.0 + 0.02 * jax.random.normal(ks[1], (DEPTH, D_MODEL), f32)
    post_g = 1.0 + 0.02 * jax.random.normal(ks[2], (DEPTH, D_MODEL), f32)
    w_in = jax.random.normal(ks[3], (DEPTH, D_MODEL, D_IN), f32) * D_MODEL ** -0.5
    ln_g = 1.0 + 0.02 * jax.random.normal(ks[4], (DEPTH, D_A), f32)
    ln_b = 0.02 * jax.random.normal(ks[5], (DEPTH, D_A), f32)
    w_s = jax.random.normal(ks[6], (DEPTH, N_HEADS_A, CHUNK, CHUNK), f32) * CHUNK ** -0.5
    b_s = 0.02 * jax.random.normal(ks[7], (DEPTH, N_HEADS_A, CHUNK), f32)
    w_f = jax.random.normal(ks[8], (DEPTH, N_GROUPS_B, GROUP_DIM_B, GROUP_DIM_B), f32) * GROUP_DIM_B ** -0.5
    b_f = 0.02 * jax.random.normal(ks[9], (DEPTH, N_GROUPS_B, GROUP_DIM_B), f32)
    w_out = jax.random.normal(ks[10], (DEPTH, D_MIX, D_MODEL), f32) * D_MIX ** -0.5
    return {"x": x, "pre_g": pre_g, "post_g": post_g, "w_in": w_in,
            "ln_g": ln_g, "ln_b": ln_b, "w_s": w_s, "b_s": b_s,
            "w_f": w_f, "b_f": b_f, "w_out": w_out}


def reference(x, pre_g, post_g, w_in, ln_g, ln_b, w_s, b_s, w_f, b_f, w_out):
    for l in range(DEPTH):
        h = rms_norm(x, pre_g[l])
        p = jnp.einsum('bsd,de->bse', h, w_in[l])
        u_a, v_a, g_a, z_b, g_b = jnp.split(
            p, [D_A, 2 * D_A, 3 * D_A, 3 * D_A + D_B], axis=-1)
        a = spatial_gating(jax.nn.gelu(u_a, approximate=False),
                           jax.nn.gelu(v_a, approximate=False),
                           w_s[l], b_s[l], ln_g[l], ln_b[l]) * jax.nn.silu(g_a)
        b = fourier_mix(z_b, w_f[l], b_f[l]) * jax.nn.silu(g_b)
        y = jnp.einsum('bse,ed->bsd', jnp.concatenate([a, b], axis=-1), w_out[l])
        x = x + rms_norm(y, post_g[l])
    return x
```

```python
import numpy as np
import ml_dtypes
from contextlib import ExitStack

import concourse.bass as bass
import concourse.mybir as mybir
from concourse.bass_utils import run_bass_kernel_spmd

F32 = mybir.dt.float32
BF16 = mybir.dt.bfloat16
AF = mybir.ActivationFunctionType
ALU = mybir.AluOpType

N_CORES = 8
B_TOTAL = 32
SEQ = 2048
D = 1024
D_IN = 2560
EPS = 1e-6
NJ = 257
NJP = 258
ENGS = ("pe", "act", "dve", "pool", "sp")


class Sched:
    def __init__(self):
        self.ops = {e: [] for e in ENGS}
        self.cnt = {}
        self.semh = {}

    def emit(self, eng, fn, waits=(), sem=None, amt=None):
        ev = None
        if sem is True:
            sem = eng
        if sem is not None:
            if amt is None:
                amt = 1
            self.cnt[sem] = self.cnt.get(sem, 0) + amt
            ev = (sem, self.cnt[sem])
        ws = []
        for w in waits:
            if w is None:
                continue
            if isinstance(w, list):
                ws.extend([q for q in w if q is not None])
            else:
                ws.append(w)
        self.ops[eng].append((fn, ws, (sem, amt) if sem is not None else None))
        return ev

    def replay(self, eng, h):
        seen = {}
        for fn, ws, inc in self.ops[eng]:
            need = {}
            for (s, v) in ws:
                if seen.get(s, 0) >= v:
                    continue
                need[s] = max(need.get(s, 0), v)
            items = list(need.items())
            for (s, v) in items[:-1]:
                h.wait_ge(self.semh[s], v)
                seen[s] = v
            ins = fn(h)
            if items:
                s, v = items[-1]
                ins._wait_ge(self.semh[s], v)
                seen[s] = v
            if inc is not None:
                ins.then_inc(self.semh[inc[0]], inc[1])


class Ring:
    def __init__(self, n):
        self.n = n
        self.i = 0
        self.free = [[] for _ in range(n)]

    def acquire(self):
        s = self.i % self.n
        self.i += 1
        w = self.free[s]
        self.free[s] = []
        return s, w

    def release(self, s, evs):
        self.free[s] = [e for e in evs if e is not None]


def build_program(NB, debug=False):
    nc = bass.Bass("TRN2", target_bir_lowering=False)

    def din(name, shape, dt=F32):
        return nc.dram_tensor(name, list(shape), dt, kind="ExternalInput").ap()

    x_h = din("x", [NB, SEQ, D])
    win_h = din("w_in", [D, D_IN])
    wout_h = din("w_out", [D, D])
    preg_h = din("preg_col", [128, 8])
    postg_h = din("postg_bc", [128, D])
    lng_h = din("lng_col", [128, 4])
    lnb_h = din("lnb_col", [128, 4])
    wsT_h = din("wsT", [128, 512])
    bs_h = din("bs_bc", [128, 512])
    wf_h = din("wf", [128, 512])
    bf_h = din("bf_col", [128, 4])
    csc_h = din("csc", [128, 384])
    ident_h = din("ident", [128, 128], BF16)
    da_h = din("da", [128, 4 * 4 * 2 * NJP], BF16)
    out_h = nc.dram_tensor("out", [NB, SEQ, D], F32, kind="ExternalOutput").ap()

    S = Sched()
    es = ExitStack()
    with es:
        def sb(name, shape, dt):
            return es.enter_context(nc.sbuf_tensor("s_" + name, list(shape), dt))

        def ps(name, shape, dt):
            return es.enter_context(nc.psum_tensor("p_" + name, list(shape), dt))

        wi = sb("wi", [128, 8, D_IN], BF16)
        wo = sb("wo", [128, 8, D], BF16)
        DA = sb("DA", [128, 4, 4, 2, NJP], BF16)
        WB = sb("WB", [128, 4, 4, 128], BF16)
        wsTb = sb("wsTb", [128, 4, 128], BF16)
        cst = sb("cst", [128, 4, 128], F32)
        postg = sb("postg", [128, D], F32)
        ident = sb("ident", [128, 128], BF16)
        cols = sb("cols", [128, 32], F32)
        stt = sb("stt", [128, 256], F32)
        hT = [sb(f"hT{i}", [128, 8, 512], BF16) for i in range(2)]
        z_sb = sb("z_sb", [128, 16, 512], BF16)
        vnG = sb("vnG", [128, 16, 512], BF16)
        guT = sb("guT", [128, 4, SEQ], BF16)
        sgbT = sb("sgbT", [128, 4, SEQ], BF16)
        xbuf = [sb(f"xbuf{i}", [128, D], F32) for i in range(4)]
        xb = [sb(f"xb{i}", [128, D], BF16) for i in range(2)]
        gv = [sb(f"gv{i}", [128, 512], F32) for i in range(4)]
        sga = [sb(f"sga{i}", [128, 512], BF16) for i in range(4)]
        gt = [sb(f"gt{i}", [128, 512], F32) for i in range(4)]
        pT = [ps(f"pT{i}", [128, 8, 128], BF16) for i in range(2)]
        pM = [ps(f"pM{i}", [128, 512], F32) for i in range(6)]

        zf = z_sb[:].rearrange("p a b -> p (a b)").bitcast(F32)
        vf = vnG[:].rearrange("p a b -> p (a b)").bitcast(F32)
        guf = guT[:].rearrange("p a b -> p (a b)").bitcast(F32)
        ot_buf = [zf[:, i * 1024:(i + 1) * 1024] for i in range(4)]
        xr_buf = [vf[:, i * 1024:(i + 1) * 1024] for i in range(4)]
        preg = cols[:, 0:8]
        lng = cols[:, 8:12]
        lnb = cols[:, 12:16]
        bfc = cols[:, 16:20]
        c_e1024 = cols[:, 20:21]
        c_eps = cols[:, 21:22]
        c_mh = cols[:, 22:23]

        pm_ring = Ring(6)
        pt_ring = Ring(2)
        ht_ring = Ring(2)
        xs_ring = Ring(3)
        xr_ring = Ring(4)
        ot_ring = Ring(4)
        xb_ring = Ring(2)
        gv_ring = Ring(4)
        sga_ring = Ring(4)
        gt_ring = Ring(4)
        dma_n = [0]

        def dma_load(dst, src, waits=(), sem=None):
            if sem is None:
                sem = f"dl{dma_n[0]}"
                dma_n[0] += 1
            return S.emit("sp", lambda h: h.dma_start(out=dst, in_=src), waits=waits, sem=sem, amt=16)

        e_cols = [
            dma_load(preg, preg_h[:, :]),
            dma_load(lng, lng_h[:, :]),
            dma_load(lnb, lnb_h[:, :]),
            dma_load(bfc, bf_h[:, :]),
        ]
        e_c1 = S.emit("pool", lambda h: h.memset(c_e1024, 1024.0 * EPS), sem=True)
        e_c2 = S.emit("pool", lambda h: h.memset(c_eps, EPS), sem=True)
        e_c3 = S.emit("pool", lambda h: h.memset(c_mh, -0.5), sem=True)
        e_stz = S.emit("pool", lambda h: h.memset(stt[:], 0.0), sem=True)
        e_pconst = [e_c1, e_c2, e_c3, e_stz]
        e_ident = dma_load(ident[:], ident_h[:, :])
        late = {}

        def load_late():
            late["da"] = dma_load(DA[:].rearrange("p a b c d -> p (a b c d)"), da_h[:, :])
            late["pg"] = dma_load(postg[:], postg_h[:, :])

        def load_late2():
            late["pg32"] = S.emit("dve", lambda h: h.tensor_scalar(out=postg[:], in0=postg[:], scalar1=32.0, scalar2=None, op0=ALU.mult),
                                  waits=[late["pg"]], sem=True)
        e_bs = dma_load(cst[:].rearrange("p a b -> p (a b)"), bs_h[:, :])
        e_csc = dma_load(gv[0][:, 0:384], csc_h[:, :])
        e_wf = dma_load(gv[1][:], wf_h[:, :])
        cc = gv[0][:, 0:128]
        scm = gv[0][:, 128:256]
        ones = gv[0][:, 256:384]

        e_wcol = {}
        e_wk = {}
        sgf = sgbT[:].rearrange("p a b -> p (a b)").bitcast(F32)
        stg = [sgf[:, i * 512:(i + 1) * 512] for i in range(8)] + [guf[:, i * 512:(i + 1) * 512] for i in range(1, 8)]
        st_ring = Ring(15)
        st_sems = [f"di{i}" for i in range(15)]
        stg_last = []
        win_pending = {}

        def win_load(c0, tag):
            lst = []
            for k in range(8):
                s_, w_ = st_ring.acquire()
                e_ld_ = dma_load(stg[s_], win_h[k * 128:(k + 1) * 128, c0:c0 + 512], waits=[w_], sem=st_sems[s_])
                lst.append((s_, e_ld_))
            win_pending[tag] = (c0, lst)

        def win_cast(tag):
            c0, lst = win_pending.pop(tag)
            e_ = None
            for k, (s_, e_ld_) in enumerate(lst):
                e_ = S.emit("dve", lambda h, k=k, s_=s_: h.tensor_scalar(
                    out=wi[:, k, c0:c0 + 512], in0=stg[s_], scalar1=preg[:, k:k + 1], scalar2=None, op0=ALU.mult),
                    waits=[e_ld_, e_cols[0]], sem=True)
                e_wk[(tag, k)] = e_
                st_ring.release(s_, [e_])
            e_wcol[tag] = e_
            stg_last.append(e_)

        wo_state = {"last": None}

        def wout_load_k(k):
            wo_state["ld"] = dma_load(xbuf[3][:], wout_h[k * 128:(k + 1) * 128, :], waits=[wo_state["last"]], sem="dwo")

        def wout_cast_k(k):
            e_cv_ = S.emit("act", lambda h: h.activation(out=wo[:, k, :], in_=xbuf[3][:], func=AF.Copy), waits=[wo_state["ld"]], sem=True)
            wo_state["last"] = e_cv_
            e_wcol["wo"] = e_cv_

        deferred = []
        for k in range(8):
            deferred.append(lambda k=k: wout_load_k(k))
            deferred.append(None)
            deferred.append(lambda k=k: wout_cast_k(k))

        e_ld = dma_load(guf[:, 0:512], wsT_h[:, :])
        e_wsb = S.emit("act", lambda h: h.activation(out=wsTb[:].rearrange("p a b -> p (a b)"), in_=guf[:, 0:512], func=AF.Copy),
                       waits=[e_ld], sem=True)
        p0, w0 = pm_ring.acquire()
        e_mmR = S.emit("pe", lambda h, p0=p0: h.matmul(pM[p0][:], lhsT=ones, rhs=guf[:, 0:512], start=True, stop=True),
                       waits=[e_ld, e_csc, w0], sem=True)
        e_cst = None
        for hh in range(4):
            e_cst = S.emit("dve", lambda h, hh=hh, p0=p0: h.scalar_tensor_tensor(
                out=cst[:, hh, :], in0=pM[p0][:, hh * 128:(hh + 1) * 128], scalar=lnb[:, hh:hh + 1], in1=cst[:, hh, :],
                op0=ALU.mult, op1=ALU.add), waits=[e_mmR, e_bs, e_cols[2]], sem=True)
        pm_ring.release(p0, [e_cst])
        e_wb = []
        for ci, cm in enumerate((cc, scm)):
            p1, w1 = pm_ring.acquire()
            e_mm = None
            for g in range(4):
                e_mm = S.emit("pe", lambda h, g=g, p1=p1, cm=cm: h.matmul(
                    pM[p1][:, g * 128:(g + 1) * 128], lhsT=cm, rhs=gv[1][:, g * 128:(g + 1) * 128], start=True, stop=True),
                    waits=[e_csc, e_wf, w1], sem=True if g == 3 else None)
            pv = pM[p1][:].rearrange("p (a b) -> p a b", a=4)
            e1 = S.emit("act", lambda h, pv=pv, ci=ci: h.activation(out=WB[:, :, 2 * ci, :], in_=pv, func=AF.Copy),
                        waits=[e_mm], sem=True)
            e2 = S.emit("act", lambda h, pv=pv, ci=ci: h.activation(out=WB[:, :, 2 * ci + 1, :], in_=pv, func=AF.Copy, scale=-1.0),
                        waits=[e_mm], sem=True)
            pm_ring.release(p1, [e2])
            e_wb.append(e2)
        gv_ring.free[0] = [e_mmR] + e_wb
        gv_ring.free[1] = list(e_wb)
        e_gate_c = [e_cst, e_wsb, e_cols[1]]
        e_B_c = e_wb + [e_cols[3]]
        e_gate_c = e_gate_c + [e_mmR]

        stat_rings = {}

        def stat_col(kind, width=1, depth=8):
            if kind not in stat_rings:
                base = sum(r[1] * r[2] for r in stat_rings.values())
                stat_rings[kind] = (Ring(depth), width, depth, base)
            ring, width, depth, base = stat_rings[kind]
            s, w = ring.acquire()
            c0 = base + s * width
            assert c0 + width <= 256
            return (c0, s, w)

        def stat_release(kind, s, evs):
            stat_rings[kind][0].release(s, evs)

        n_units = NB * 16
        s1_state = {}

        def s1_load(n):
            if n >= n_units or n in s1_state:
                return
            b, i = divmod(n, 16)
            s, w = xs_ring.acquire()
            e = dma_load(xbuf[s][:], x_h[b, i * 128:(i + 1) * 128, :], waits=[w], sem=f"dxs{s}")
            s1_state[n] = (s, e)

        ht_cur = {}

        s1_mid = {}

        def s1a(n):
            b, i = divmod(n, 16)
            blk, ii = divmod(i, 4)
            gblk = b * 4 + blk
            s, e_ld = s1_state.pop(n)
            q, qw = xb_ring.acquire()
            c_ss, k_ss, w_ss = stat_col("ssx")
            c_t, k_t, w_t = stat_col("tx")
            c_r, k_r, w_r = stat_col("rx")
            e_sq = S.emit("act", lambda h: h.activation(out=xb[q][:], in_=xbuf[s][:], func=AF.Square,
                                                        accum_out=stt[:, c_ss:c_ss + 1]),
                          waits=[e_ld, qw, w_ss, e_pconst], sem=True)
            e_t = S.emit("pool", lambda h: h.tensor_tensor(out=stt[:, c_t:c_t + 1], in0=stt[:, c_ss:c_ss + 1], in1=c_e1024, op=ALU.add),
                         waits=[e_sq, w_t, e_pconst], sem=True)
            e_r = S.emit("pool", lambda h: h.tensor_tensor(out=stt[:, c_r:c_r + 1], in0=stt[:, c_t:c_t + 1], in1=c_mh, op=ALU.pow),
                         waits=[e_t, w_r], sem=True)
            e_cv = S.emit("dve", lambda h: h.tensor_scalar(out=xb[q][:], in0=xbuf[s][:], scalar1=stt[:, c_r:c_r + 1], scalar2=32.0,
                                                           op0=ALU.mult, op1=ALU.mult),
                          waits=[e_r, e_sq], sem=True)
            stat_release("ssx", k_ss, [e_t])
            stat_release("tx", k_t, [e_r])
            stat_release("rx", k_r, [e_cv])
            if s < 3:
                xs_ring.release(s, [e_cv])
            else:
                wo_state["last"] = e_cv
            s1_load(n + 3)
            s1_mid[n] = (q, e_cv)

        def s1b(n):
            q, e_cv = s1_mid.pop(n)
            t, tw = pt_ring.acquire()
            e_tr = None
            for k in range(8):
                e_tr = S.emit("pe", lambda h, k=k: h.transpose(out=pT[t][:, k, :], in_=xb[q][:, k * 128:(k + 1) * 128], identity=ident[:]),
                              waits=[e_cv, tw, e_ident], sem=True if k == 7 else None)
            xb_ring.release(q, [e_tr])
            s1_mid[("c", n)] = (t, e_tr)

        def s1c(n):
            b, i = divmod(n, 16)
            blk, ii = divmod(i, 4)
            gblk = b * 4 + blk
            if ii == 0:
                hs, hw = ht_ring.acquire()
                ht_cur[gblk] = [hs, hw, None]
            hs, hw, _ = ht_cur[gblk]
            t, e_tr = s1_mid.pop(("c", n))
            e_ev = S.emit("act", lambda h: h.activation(out=hT[hs][:, :, ii * 128:(ii + 1) * 128], in_=pT[t][:, :, :], func=AF.Copy),
                          waits=[e_tr, hw], sem=True)
            pt_ring.release(t, [e_ev])
            ht_cur[gblk][2] = e_ev

        def mm_group(pbank_ap_list, specs, waits):
            e = None
            n = len(specs)
            for j, spec in enumerate(specs):
                oap, lhsT, rhs, st, sp_ = spec[:5]
                xw = [spec[5]] if len(spec) > 5 else []
                e = S.emit("pe", lambda h, oap=oap, lhsT=lhsT, rhs=rhs, st=st, sp_=sp_: h.matmul(oap, lhsT=lhsT, rhs=rhs, start=st, stop=sp_),
                           waits=(list(waits) + xw) if j == 0 else xw, sem=True if j == n - 1 else None)
            return e

        last_s2 = {}
        scratch_done = {}
        vn_ready = {}
        gu_ready = {}
        sgb_ready = {}
        z_ready = {}

        def s2_tasks(b, blk):
            gblk = b * 4 + blk
            hs, _, e_h = ht_cur[gblk]
            H = hT[hs]
            tok = slice(blk * 512, (blk + 1) * 512)
            tasks = []

            def z_task(r):
                p, w = pm_ring.acquire()
                e_mm = mm_group(None, [(pM[p][:], H[:, k, r:512:4], wi[:, k, 1536:2048], k == 0, k == 7, e_wk[("z", k)]) for k in range(8)],
                                [e_h, w])
                e_ev = S.emit("act", lambda h: h.activation(out=z_sb[:, blk * 4 + r, :], in_=pM[p][:], func=AF.Copy),
                              waits=[e_mm, z_ready.get(("A", b - 1)), scratch_done.get((b - 1, "z", blk))], sem=True)
                pm_ring.release(p, [e_ev])
                z_ready[(b, blk, r)] = e_ev
                last_s2[gblk] = e_mm

            def v_task(j):
                c = blk * 4 + j
                p, w = pm_ring.acquire()
                e_mm = mm_group(None, [(pM[p][:], H[:, k, j * 128:(j + 1) * 128], wi[:, k, 512:1024], k == 0, k == 7, e_wk[("v", k)]) for k in range(8)],
                                [e_h, w])
                g_, gw = gv_ring.acquire()
                e_g = S.emit("act", lambda h: h.activation(out=gv[g_][:], in_=pM[p][:], func=AF.Gelu), waits=[e_mm, gw], sem=True)
                pm_ring.release(p, [e_g])
                c_bs, k_bs, w_bs = stat_col("bns", 6)
                c_mv, k_mv, w_mv = stat_col("mv", 2)
                c_t, k_t, w_t = stat_col("tv")
                c_r, k_r, w_r = stat_col("rv")
                e_bs_ = S.emit("dve", lambda h: h.bn_stats(out=stt[:, c_bs:c_bs + 6], in_=gv[g_][:]), waits=[e_g, w_bs], sem=True)
                e_ba = S.emit("dve", lambda h: h.bn_aggr(out=stt[:, c_mv:c_mv + 2], in_=stt[:, c_bs:c_bs + 6]), waits=[e_bs_, w_mv], sem=True)
                e_t = S.emit("pool", lambda h: h.tensor_tensor(out=stt[:, c_t:c_t + 1], in0=stt[:, c_mv + 1:c_mv + 2], in1=c_eps, op=ALU.add),
                             waits=[e_ba, w_t], sem=True)
                e_r = S.emit("pool", lambda h: h.tensor_tensor(out=stt[:, c_r:c_r + 1], in0=stt[:, c_t:c_t + 1], in1=c_mh, op=ALU.pow),
                             waits=[e_t, w_r], sem=True)
                e_n = S.emit("dve", lambda h: h.tensor_scalar(out=vnG[:, c, :], in0=gv[g_][:], scalar1=stt[:, c_mv:c_mv + 1],
                                                              scalar2=stt[:, c_r:c_r + 1], op0=ALU.subtract, op1=ALU.mult),
                             waits=[e_r, e_ba, vn_ready.get(("B", b - 1)), scratch_done.get((b - 1, "v", blk))], sem=True)
                gv_ring.release(g_, [e_n])
                stat_release("bns", k_bs, [e_ba])
                stat_release("mv", k_mv, [e_n])
                stat_release("tv", k_t, [e_r])
                stat_release("rv", k_r, [e_n])
                vn_ready[(b, c)] = e_n
                last_s2[gblk] = e_mm

            def u_task(e):
                p, w = pm_ring.acquire()
                e_mm = mm_group(None, [(pM[p][:], wi[:, k, e * 128:(e + 1) * 128], H[:, k, :], k == 0, k == 7, e_wk[("u", k)]) for k in range(8)],
                                [e_h, w])
                e_g = S.emit("act", lambda h: h.activation(out=guT[:, e, tok], in_=pM[p][:], func=AF.Gelu),
                             waits=[e_mm, gu_ready.get(("W", b - 1)), e_gate_c, stg_last], sem=True)
                pm_ring.release(p, [e_g])
                gu_ready[(b, blk, e, "u")] = e_g
                last_s2[gblk] = e_mm

            def ga_task(e):
                p, w = pm_ring.acquire()
                e_mm = mm_group(None, [(pM[p][:], wi[:, k, 1024 + e * 128:1024 + (e + 1) * 128], H[:, k, :], k == 0, k == 7, e_wk[("ga", k)]) for k in range(8)],
                                [e_h, w])
                a_, aw = sga_ring.acquire()
                e_s = S.emit("act", lambda h: h.activation(out=sga[a_][:], in_=pM[p][:], func=AF.Silu), waits=[e_mm, aw], sem=True)
                pm_ring.release(p, [e_s])
                e_m = S.emit("pool", lambda h: h.tensor_tensor(out=guT[:, e, tok], in0=guT[:, e, tok], in1=sga[a_][:], op=ALU.mult),
                             waits=[e_s, gu_ready[(b, blk, e, "u")]], sem=True)
                sga_ring.release(a_, [e_m])
                gu_ready[(b, blk, e)] = e_m
                last_s2[gblk] = e_mm

            def gb_task(e):
                p, w = pm_ring.acquire()
                e_mm = mm_group(None, [(pM[p][:], wi[:, k, 2048 + e * 128:2048 + (e + 1) * 128], H[:, k, :], k == 0, k == 7, e_wk[("gb", k)]) for k in range(8)],
                                [e_h, w])
                e_s = S.emit("act", lambda h: h.activation(out=sgbT[:, e, tok], in_=pM[p][:], func=AF.Silu),
                             waits=[e_mm, gu_ready.get(("W", b - 1)), stg_last], sem=True)
                pm_ring.release(p, [e_s])
                sgb_ready[(b, blk, e)] = e_s
                last_s2[gblk] = e_mm

            def gate_task(hh):
                p, w = pm_ring.acquire()
                specs = [(pM[p][:, j * 128:(j + 1) * 128], vnG[:, blk * 4 + j, hh * 128:(hh + 1) * 128], wsTb[:, hh, :], True, True)
                         for j in range(4)]
                e_mm = mm_group(None, specs, [vn_ready[(b, blk * 4 + 3)], vn_ready[(b, blk * 4 + 2)], vn_ready[(b, blk * 4 + 1)],
                                              vn_ready[(b, blk * 4)], w, e_gate_c])
                pv = pM[p][:].rearrange("p (a b) -> p a b", a=4)
                cb = cst[:, hh, :].unsqueeze(1).broadcast_to([128, 4, 128])
                t_, tw_ = gt_ring.acquire()
                gv_ = gt[t_][:].rearrange("p (a b) -> p a b", a=4)
                e_1 = S.emit("dve", lambda h: h.scalar_tensor_tensor(out=gv_, in0=pv, scalar=lng[:, hh:hh + 1], in1=cb,
                                                                    op0=ALU.mult, op1=ALU.add), waits=[e_mm, e_gate_c, tw_], sem=True)
                pm_ring.release(p, [e_1])
                e_2 = S.emit("pool", lambda h: h.tensor_tensor(out=guT[:, hh, tok], in0=gt[t_][:], in1=guT[:, hh, tok], op=ALU.mult),
                             waits=[e_1, gu_ready[(b, blk, hh)]], sem=True)
                gt_ring.release(t_, [e_2])
                gu_ready[(b, blk, hh, "a")] = e_2
                vn_ready[("G", b, blk)] = e_mm

            for r in range(4):
                tasks.append(lambda r=r: z_task(r))
            for j in range(4):
                tasks.append(lambda j=j: v_task(j))
            for e in range(4):
                tasks.append(lambda e=e: u_task(e))
            for e in range(4):
                tasks.append(lambda e=e: ga_task(e))
            for e in range(4):
                tasks.append(lambda e=e: gb_task(e))
            for hh in range(4):
                tasks.append(lambda hh=hh: gate_task(hh))
            return tasks

        def dft_tasks(b):
            tasks = []
            g_ev = {}

            def a_task(g, r):
                gbuf = g % 2
                G = lambda j: vnG[:, gbuf * 8 + j, 0:NJ]
                p0_, w0_ = pm_ring.acquire()
                p1_, w1_ = pm_ring.acquire()
                specs = []
                for t in range(4):
                    lhsT = z_sb[:, t * 4 + r, g * 128:(g + 1) * 128]
                    specs.append((pM[p0_][:, 0:NJ], lhsT, DA[:, r, t, 0, 0:NJ], t == 0, t == 3))
                    specs.append((pM[p1_][:, 0:NJ], lhsT, DA[:, r, t, 1, 0:NJ], t == 0, t == 3))
                zw = [z_ready[(b, t, r)] for t in range(4)]
                gw = [vn_ready[("G", b, blk)] for blk in range(4)]
                e_mm = mm_group(None, specs, zw + [w0_, w1_, late["da"]])
                wprev = [g_ev.get(("B", g - 2))]
                if r < 2:
                    e0 = S.emit("act", lambda h: h.activation(out=G(2 * r), in_=pM[p0_][:, 0:NJ], func=AF.Copy),
                                waits=[e_mm] + gw + wprev, sem=True)
                    e1 = S.emit("act", lambda h: h.activation(out=G(2 * r + 1), in_=pM[p1_][:, 0:NJ], func=AF.Copy),
                                waits=[e_mm] + gw + wprev, sem=True)
                    pm_ring.release(p0_, [e0])
                    pm_ring.release(p1_, [e1])
                    g_ev[("A", g, r)] = [e0, e1]
                else:
                    src = g_ev[("A", g, r - 2)]
                    evs = []
                    for cs, pb in ((0, p0_), (1, p1_)):
                        old = 2 * (r - 2) + cs
                        new_ = 4 + 2 * (r - 2) + cs
                        e_s = S.emit("dve", lambda h, pb=pb, old=old, new_=new_: h.tensor_tensor(
                            out=G(new_), in0=pM[pb][:, 0:NJ], in1=G(old), op=ALU.add), waits=[e_mm] + src + gw + wprev, sem=True)
                        e_d = S.emit("dve", lambda h, pb=pb, old=old: h.tensor_tensor(
                            out=G(old), in0=pM[pb][:, 0:NJ], in1=G(old), op=ALU.subtract), waits=[e_s], sem=True)
                        pm_ring.release(pb, [e_d])
                        evs += [e_s, e_d]
                    g_ev[("A", g, r)] = evs
                z_ready[("A", b)] = e_mm

            BSPEC = {0: ((0, 4), (0, 6), (3, 5), (3, 7)),
                     2: ((0, 4), (1, 6), (3, 5), (2, 7)),
                     1: ((1, 0), (0, 3), (2, 1), (2, 2)),
                     3: ((1, 0), (1, 3), (2, 1), (3, 2))}

            BSPEC_UP = {0: ((1, 0), (1, 3), (3, 1), (2, 2)),
                        2: ((1, 0), (0, 3), (3, 1), (3, 2)),
                        1: ((0, 4), (1, 6), (2, 5), (3, 7)),
                        3: ((0, 4), (0, 6), (2, 5), (2, 7))}

            def b_task(g, m):
                gbuf = g % 2
                p, w = pm_ring.acquire()
                specs = []
                for n, (idx, slot) in enumerate(BSPEC[m]):
                    specs.append((pM[p][:, 0:NJ], WB[:, g, idx, :], vnG[:, gbuf * 8 + slot, 0:NJ], n == 0, n == 3))
                for n, (idx, slot) in enumerate(BSPEC_UP[m]):
                    specs.append((pM[p][:, NJ:512], WB[:, g, idx, :], vnG[:, gbuf * 8 + slot, 255:0:-1], n == 0, n == 3))
                aw = []
                for r in range(4):
                    aw += g_ev[("A", g, r)]
                e_mm = mm_group(None, specs, aw + [w, e_B_c])
                ks = slice(m * 512, (m + 1) * 512)
                sw = [sgb_ready[(b, m, g)]]
                e_o = S.emit("dve", lambda h: h.scalar_tensor_tensor(out=sgbT[:, g, ks], in0=pM[p][:], scalar=bfc[:, g:g + 1],
                                                                    in1=sgbT[:, g, ks], op0=ALU.add, op1=ALU.mult),
                             waits=[e_mm] + sw + [e_B_c], sem=True)
                pm_ring.release(p, [e_o])
                g_ev[("B", g)] = e_mm
                vn_ready[("B", b)] = e_mm
                sgb_ready[(b, "b", g, m)] = e_o

            order = [("A", 0), ("A", 1), ("B", 0), ("A", 2), ("B", 1), ("A", 3), ("B", 2), ("B", 3)]
            for kind, g in order:
                for j in range(4):
                    if kind == "A":
                        tasks.append(lambda g=g, j=j: a_task(g, j))
                    else:
                        tasks.append(lambda g=g, j=j: b_task(g, j))
            return tasks

        store_evs = []

        def wout_tasks(b):
            tasks = []
            mid = {}

            def w_front(i):
                s, w = xr_ring.acquire()
                xr = xr_buf[s]
                e_ld = dma_load(xr, x_h[b, i * 128:(i + 1) * 128, :], waits=[w, vn_ready[("B", b)]], sem=f"dxr{s}")
                p0_, w0_ = pm_ring.acquire()
                p1_, w1_ = pm_ring.acquire()
                pp = (p0_, p1_)
                ts_ = slice(i * 128, (i + 1) * 128)
                specs = []
                for e in range(8):
                    lhsT = guT[:, e, ts_] if e < 4 else sgbT[:, e - 4, ts_]
                    for dh in range(2):
                        specs.append((pM[pp[dh]][:], lhsT, wo[:, e, dh * 512:(dh + 1) * 512], e == 0, e == 7))
                blk = i // 4
                aw = [gu_ready[(b, blk, hh, "a")] for hh in range(4)] + [sgb_ready[(b, "b", g, blk)] for g in range(4)]
                e_mm = mm_group(None, specs, aw + [w0_, w1_, e_wcol["wo"]])
                gu_ready[("W", b)] = e_mm
                o_, ow = ot_ring.acquire()
                ot = ot_buf[o_]
                c_ss, k_ss, w_ss = stat_col("ssy", 2)
                c_s, k_s, w_s = stat_col("sy")
                c_t, k_t, w_t = stat_col("ty")
                c_r, k_r, w_r = stat_col("ry")
                e_sq = None
                for dh in range(2):
                    cs_ = slice(dh * 512, (dh + 1) * 512)
                    j_, jw = gv_ring.acquire()
                    e_sq = S.emit("act", lambda h, dh=dh, j_=j_: h.activation(out=gv[j_][:], in_=pM[pp[dh]][:], func=AF.Square,
                                                                            accum_out=stt[:, c_ss + dh:c_ss + dh + 1]),
                                  waits=[e_mm, jw, w_ss], sem=True)
                    gv_ring.release(j_, [e_sq])
                    e_a = S.emit("dve", lambda h, dh=dh, cs_=cs_: h.tensor_tensor(out=ot[:, cs_], in0=pM[pp[dh]][:], in1=postg[:, cs_], op=ALU.mult),
                                 waits=[e_mm, e_sq, ow, z_ready[("A", b)], late["pg32"]], sem=True)
                    pm_ring.release(pp[dh], [e_sq, e_a])
                e_s = S.emit("pool", lambda h: h.tensor_tensor(out=stt[:, c_s:c_s + 1], in0=stt[:, c_ss:c_ss + 1], in1=stt[:, c_ss + 1:c_ss + 2], op=ALU.add),
                             waits=[e_sq, w_s], sem=True)
                e_t = S.emit("pool", lambda h: h.tensor_tensor(out=stt[:, c_t:c_t + 1], in0=stt[:, c_s:c_s + 1], in1=c_e1024, op=ALU.add),
                             waits=[e_s, w_t], sem=True)
                e_r = S.emit("pool", lambda h: h.tensor_tensor(out=stt[:, c_r:c_r + 1], in0=stt[:, c_t:c_t + 1], in1=c_mh, op=ALU.pow),
                             waits=[e_t, w_r], sem=True)
                stat_release("ssy", k_ss, [e_s])
                stat_release("sy", k_s, [e_t])
                stat_release("ty", k_t, [e_r])
                mid[i] = (s, xr, e_ld, o_, ot, e_a, e_r, c_r, k_r)

            def w_back(i):
                s, xr, e_ld, o_, ot, e_a, e_r, c_r, k_r = mid.pop(i)
                e_fin = S.emit("dve", lambda h: h.scalar_tensor_tensor(out=xr, in0=ot, scalar=stt[:, c_r:c_r + 1], in1=xr,
                                                                      op0=ALU.mult, op1=ALU.add),
                               waits=[e_a, e_r, e_ld], sem=True)
                stat_release("ry", k_r, [e_fin])
                ot_ring.release(o_, [e_fin])
                e_st = S.emit("pool", lambda h: h.dma_start(out=out_h[b, i * 128:(i + 1) * 128, :], in_=xr),
                              waits=[e_fin], sem=f"dst{s}", amt=16)
                xr_ring.release(s, [e_st])
                store_evs.append(e_st)
                scratch_done[(b, "z", o_)] = e_fin
                scratch_done[(b, "v", s)] = e_st

            tasks.append(lambda: w_front(0))
            for i in range(1, 16):
                tasks.append(lambda i=i: (w_front(i), w_back(i - 1)))
            tasks.append(lambda: w_back(15))
            return tasks

        for n_ in range(3):
            s1_load(n_)
        s1_state[3] = (3, dma_load(xbuf[3][:], x_h[0, 3 * 128:4 * 128, :], sem="dxs3"))
        win_load(1536, "z")
        win_cast("z")
        win_load(512, "v")
        for st_, n_ in (("a", 0), ("a", 1), ("b", 0), ("a", 2), ("b", 1), ("c", 0), ("a", 3), ("b", 2), ("c", 1), ("b", 3), ("c", 2), ("c", 3)):
            {"a": s1a, "b": s1b, "c": s1c}[st_](n_)
        win_cast("v")
        sched = {0: [("a", 0)], 1: [("a", 1)], 5: [("b", 0), ("a", 2)], 6: [("b", 1)], 7: [("c", 0), ("a", 3)],
                 9: [("c", 1)], 10: [("b", 2)], 13: [("b", 3), ("c", 2)], 16: [("c", 3)]}
        STG = {"a": s1a, "b": s1b, "c": s1c}

        def pop_deferred():
            if deferred:
                f_ = deferred.pop(0)
                if f_ is not None:
                    f_()

        for b in range(NB):
            blk0 = 0
            if b == 0 and NB * 4 >= 3:
                t0 = s2_tasks(0, 0)
                sch1 = {0: [("a", 4)], 1: [("a", 5)], 2: [("b", 4)], 3: [("a", 6), ("b", 5)], 4: [("c", 4)],
                        5: [("a", 7), ("b", 6)], 6: [("c", 5)], 7: [("b", 7), ("c", 6), ("c", 7)]}
                for ti in range(8):
                    t0[ti]()
                    for st_, n_ in sch1.get(ti, ()):
                        STG[st_](n_)
                for c0_, tg_ in ((0, "u"), (1024, "ga"), (2048, "gb")):
                    win_load(c0_, tg_)
                    win_cast(tg_)
                t1 = s2_tasks(0, 1)
                for ti in range(8):
                    t1[ti]()
                merged = t0[8:] + t1[8:]
                sch2 = {2: [("a", 8)], 4: [("a", 9)], 8: [("b", 8)], 9: [("a", 10)], 10: [("b", 9)], 12: [("c", 8)],
                        13: [("a", 11)], 15: [("c", 9)], 16: [("b", 10)], 19: [("b", 11)], 20: [("c", 10)], 23: [("c", 11)]}
                for mi, t in enumerate(merged):
                    t()
                    if mi == 11:
                        ht_ring.release(ht_cur[0][0], [last_s2[0]])
                    if mi == 15:
                        load_late()
                    if 8 * 4 < n_units or True:
                        for st_, n_ in sch2.get(mi, ()):
                            if n_ < n_units:
                                STG[st_](n_)
                    if mi >= 16 and mi % 2 == 1:
                        pop_deferred()
                ht_ring.release(ht_cur[1][0], [last_s2[1]])
                load_late2()
                blk0 = 2
            for blk in range(blk0, 4):
                gblk = b * 4 + blk
                tasks = s2_tasks(b, blk)
                nxt = (gblk + 1) * 4
                for ti, t in enumerate(tasks):
                    t()
                    if nxt < n_units:
                        for st_, j_ in sched.get(ti, ()):
                            STG[st_](nxt + j_)
                    if b == 0 and ti % 2 == 1:
                        pop_deferred()
                hs = ht_cur[gblk][0]
                ht_ring.release(hs, [last_s2[gblk]])
            while deferred:
                f_ = deferred.pop(0)
                if f_ is not None:
                    f_()
            for t in dft_tasks(b):
                t()
            for t in wout_tasks(b):
                t()
        S.emit("pool", lambda h: h.nop(), waits=store_evs[-6:])
        if debug:
            allev = [(n_, S.cnt[n_]) for n_ in ("pe", "act", "dve", "pool")] + store_evs[-6:]
            dbg = [("d_wi", wi[:].rearrange("p a b -> p (a b)"), BF16), ("d_wo", wo[:].rearrange("p a b -> p (a b)"), BF16),
                   ("d_WB", WB[:].rearrange("p a b c -> p (a b c)"), BF16), ("d_cst", cst[:].rearrange("p a b -> p (a b)"), F32),
                   ("d_hT0", hT[0][:].rearrange("p a b -> p (a b)"), BF16), ("d_hT1", hT[1][:].rearrange("p a b -> p (a b)"), BF16),
                   ("d_z", z_sb[:].rearrange("p a b -> p (a b)"), BF16), ("d_vnG", vnG[:].rearrange("p a b -> p (a b)"), BF16),
                   ("d_guT", guT[:].rearrange("p a b -> p (a b)"), BF16), ("d_sgbT", sgbT[:].rearrange("p a b -> p (a b)"), BF16),
                   ("d_stt", stt[:], F32), ("d_wsTb", wsTb[:].rearrange("p a b -> p (a b)"), BF16), ("d_postg", postg[:], F32)]
            evs = []
            for (nm, ap_, dt_) in dbg:
                dh_ = nc.dram_tensor(nm, [128, ap_.shape[1]], dt_, kind="ExternalOutput").ap()
                evs.append(S.emit("sp", lambda h, dh_=dh_, ap_=ap_: h.dma_start(out=dh_[:, :], in_=ap_), waits=allev, sem="ddbg", amt=16))
            S.emit("sp", lambda h: h.nop(), waits=[evs[-1]])

        for sname in sorted(S.cnt.keys()):
            S.semh[sname] = es.enter_context(nc.semaphore(sname))
        block = es.enter_context(nc.Block())

        @block.sync
        def _(h):
            S.replay("sp", h)

        @block.scalar
        def _(h):
            S.replay("act", h)

        @block.vector
        def _(h):
            S.replay("dve", h)

        @block.tensor
        def _(h):
            S.replay("pe", h)

        @block.gpsimd
        def _(h):
            S.replay("pool", h)
    return nc


def host_constants():
    j = np.arange(NJP, dtype=np.int64)
    p = np.arange(128, dtype=np.int64)
    da = np.zeros((128, 4, 4, 2, NJP), dtype=np.float64)
    for r in range(4):
        for t in range(4):
            s = 4 * (128 * t + p) + r
            ang = 2.0 * np.pi * ((s[:, None] * j[None, :]) % 2048) / 2048.0
            da[:, r, t, 0, :] = np.cos(ang) / 512.0
            da[:, r, t, 1, :] = np.sin(ang) / 512.0
    c = np.arange(128, dtype=np.int64)
    angc = 2.0 * np.pi * ((c[:, None] * c[None, :]) % 128) / 128.0
    csc = np.concatenate([np.cos(angc), np.sin(angc), np.ones((128, 128))], axis=1).astype(np.float32)
    ident = np.eye(128, dtype=np.float32).astype(ml_dtypes.bfloat16)
    return (np.ascontiguousarray(da.reshape(128, -1).astype(np.float32).astype(ml_dtypes.bfloat16)),
            np.ascontiguousarray(csc), ident)


def make_in_maps(x, pre_g, post_g, w_in, ln_g, ln_b, w_s, b_s, w_f, b_f, w_out, n_cores, nb):
    f = np.float32
    da, csc, ident = host_constants()
    shared = {
        "w_in": np.ascontiguousarray(w_in[0], dtype=f),
        "w_out": np.ascontiguousarray(w_out[0], dtype=f),
        "preg_col": np.ascontiguousarray(pre_g[0].reshape(8, 128).T, dtype=f),
        "postg_bc": np.ascontiguousarray(np.broadcast_to(post_g[0][None, :], (128, D)), dtype=f),
        "lng_col": np.ascontiguousarray(ln_g[0].reshape(4, 128).T, dtype=f),
        "lnb_col": np.ascontiguousarray(ln_b[0].reshape(4, 128).T, dtype=f),
        "wsT": np.ascontiguousarray(w_s[0].transpose(2, 0, 1).reshape(128, 512), dtype=f),
        "bs_bc": np.ascontiguousarray(np.broadcast_to(b_s[0].reshape(1, 512), (128, 512)), dtype=f),
        "wf": np.ascontiguousarray(w_f[0].transpose(1, 0, 2).reshape(128, 512), dtype=f),
        "bf_col": np.ascontiguousarray(b_f[0].T, dtype=f),
        "csc": csc,
        "ident": ident,
        "da": da,
    }
    maps = []
    for c in range(n_cores):
        m = dict(shared)
        m["x"] = np.ascontiguousarray(x[c * nb:(c + 1) * nb], dtype=f)
        maps.append(m)
    return maps


_PROG = {}


def kernel(x, pre_g, post_g, w_in, ln_g, ln_b, w_s, b_s, w_f, b_f, w_out):
    args = [np.asarray(a) for a in (x, pre_g, post_g, w_in, ln_g, ln_b, w_s, b_s, w_f, b_f, w_out)]
    nb = B_TOTAL // N_CORES
    if nb not in _PROG:
        _PROG[nb] = build_program(nb)
    nc = _PROG[nb]
    in_maps = make_in_maps(*args, n_cores=N_CORES, nb=nb)
    res = run_bass_kernel_spmd(nc, in_maps, core_ids=list(range(N_CORES)))
    out = np.concatenate([np.asarray(r["out"]) for r in res.results], axis=0)
    return out.astype(np.float32, copy=False)
```

```python
import numpy as np
import ml_dtypes
from contextlib import ExitStack

import concourse.bass as bass
import concourse.mybir as mybir
from concourse.bass_utils import run_bass_kernel_spmd

F32 = mybir.dt.float32
BF16 = mybir.dt.bfloat16
AF = mybir.ActivationFunctionType
ALU = mybir.AluOpType

N_CORES = 8
B_TOTAL = 32
SEQ = 2048
D = 1024
D_IN = 2560
EPS = 1e-6
NJ = 257
NJP = 258
ENGS = ("pe", "act", "dve", "pool", "sp")


class Sched:
    def __init__(self):
        self.ops = {e: [] for e in ENGS}
        self.cnt = {}
        self.semh = {}

    def emit(self, eng, fn, waits=(), sem=None, amt=None):
        ev = None
        if sem is True:
            sem = eng
        if sem is not None:
            if amt is None:
                amt = 1
            self.cnt[sem] = self.cnt.get(sem, 0) + amt
            ev = (sem, self.cnt[sem])
        ws = []
        for w in waits:
            if w is None:
                continue
            if isinstance(w, list):
                ws.extend([q for q in w if q is not None])
            else:
                ws.append(w)
        self.ops[eng].append((fn, ws, (sem, amt) if sem is not None else None))
        return ev

    def replay(self, eng, h):
        seen = {}
        for fn, ws, inc in self.ops[eng]:
            need = {}
            for (s, v) in ws:
                if seen.get(s, 0) >= v:
                    continue
                need[s] = max(need.get(s, 0), v)
            items = list(need.items())
            for (s, v) in items[:-1]:
                h.wait_ge(self.semh[s], v)
                seen[s] = v
            ins = fn(h)
            if items:
                s, v = items[-1]
                ins._wait_ge(self.semh[s], v)
                seen[s] = v
            if inc is not None:
                ins.then_inc(self.semh[inc[0]], inc[1])


class Ring:
    def __init__(self, n):
        self.n = n
        self.i = 0
        self.free = [[] for _ in range(n)]

    def acquire(self):
        s = self.i % self.n
        self.i += 1
        w = self.free[s]
        self.free[s] = []
        return s, w

    def release(self, s, evs):
        self.free[s] = [e for e in evs if e is not None]


def build_program(NB, debug=False):
    nc = bass.Bass("TRN2", target_bir_lowering=False)

    def din(name, shape, dt=F32):
        return nc.dram_tensor(name, list(shape), dt, kind="ExternalInput").ap()

    x_h = din("x", [NB, SEQ, D])
    win_h = din("w_in", [D, D_IN])
    wout_h = din("w_out", [D, D])
    preg_h = din("preg_col", [128, 8])
    postg_h = din("postg_bc", [128, D])
    lng_h = din("lng_col", [128, 4])
    lnb_h = din("lnb_col", [128, 4])
    wsT_h = din("wsT", [128, 512])
    bs_h = din("bs_bc", [128, 512])
    wf_h = din("wf", [128, 512])
    bf_h = din("bf_col", [128, 4])
    csc_h = din("csc", [128, 384])
    ident_h = din("ident", [128, 128], BF16)
    da_h = din("da", [128, 4 * 4 * 2 * NJP], BF16)
    out_h = nc.dram_tensor("out", [NB, SEQ, D], F32, kind="ExternalOutput").ap()

    S = Sched()
    es = ExitStack()
    with es:
        def sb(name, shape, dt):
            return es.enter_context(nc.sbuf_tensor("s_" + name, list(shape), dt))

        def ps(name, shape, dt):
            return es.enter_context(nc.psum_tensor("p_" + name, list(shape), dt))

        wi = sb("wi", [128, 8, D_IN], BF16)
        wo = sb("wo", [128, 8, D], BF16)
        DA = sb("DA", [128, 4, 4, 2, NJP], BF16)
        WB = sb("WB", [128, 4, 4, 128], BF16)
        wsTb = sb("wsTb", [128, 4, 128], BF16)
        cst = sb("cst", [128, 4, 128], F32)
        postg = sb("postg", [128, D], F32)
        ident = sb("ident", [128, 128], BF16)
        cols = sb("cols", [128, 32], F32)
        stt = sb("stt", [128, 256], F32)
        hT = [sb(f"hT{i}", [128, 8, 512], BF16) for i in range(2)]
        z_sb = sb("z_sb", [128, 16, 512], BF16)
        vnG = sb("vnG", [128, 16, 512], BF16)
        guT = sb("guT", [128, 4, SEQ], BF16)
        sgbT = sb("sgbT", [128, 4, SEQ], BF16)
        xbuf = [sb(f"xbuf{i}", [128, D], F32) for i in range(4)]
        xb = [sb(f"xb{i}", [128, D], BF16) for i in range(2)]
        gv = [sb(f"gv{i}", [128, 512], F32) for i in range(4)]
        sga = [sb(f"sga{i}", [128, 512], BF16) for i in range(4)]
        gt = [sb(f"gt{i}", [128, 512], F32) for i in range(4)]
        pT = [ps(f"pT{i}", [128, 8, 128], BF16) for i in range(2)]
        pM = [ps(f"pM{i}", [128, 512], F32) for i in range(6)]

        zf = z_sb[:].rearrange("p a b -> p (a b)").bitcast(F32)
        vf = vnG[:].rearrange("p a b -> p (a b)").bitcast(F32)
        guf = guT[:].rearrange("p a b -> p (a b)").bitcast(F32)
        ot_buf = [zf[:, i * 1024:(i + 1) * 1024] for i in range(4)]
        xr_buf = [vf[:, i * 1024:(i + 1) * 1024] for i in range(4)]
        preg = cols[:, 0:8]
        lng = cols[:, 8:12]
        lnb = cols[:, 12:16]
        bfc = cols[:, 16:20]
        c_e1024 = cols[:, 20:21]
        c_eps = cols[:, 21:22]
        c_mh = cols[:, 22:23]

        pm_ring = Ring(6)
        pt_ring = Ring(2)
        ht_ring = Ring(2)
        xs_ring = Ring(3)
        xr_ring = Ring(4)
        ot_ring = Ring(4)
        xb_ring = Ring(2)
        gv_ring = Ring(4)
        sga_ring = Ring(4)
        gt_ring = Ring(4)
        dma_n = [0]

        def dma_load(dst, src, waits=(), sem=None):
            if sem is None:
                sem = f"dl{dma_n[0]}"
                dma_n[0] += 1
            return S.emit("sp", lambda h: h.dma_start(out=dst, in_=src), waits=waits, sem=sem, amt=16)

        e_cols = [
            dma_load(preg, preg_h[:, :]),
            dma_load(lng, lng_h[:, :]),
            dma_load(lnb, lnb_h[:, :]),
            dma_load(bfc, bf_h[:, :]),
        ]
        e_c1 = S.emit("pool", lambda h: h.memset(c_e1024, 1024.0 * EPS), sem=True)
        e_c2 = S.emit("pool", lambda h: h.memset(c_eps, EPS), sem=True)
        e_c3 = S.emit("pool", lambda h: h.memset(c_mh, -0.5), sem=True)
        e_stz = S.emit("pool", lambda h: h.memset(stt[:], 0.0), sem=True)
        e_pconst = [e_c1, e_c2, e_c3, e_stz]
        e_ident = dma_load(ident[:], ident_h[:, :])
        late = {}

        def load_late():
            late["da"] = dma_load(DA[:].rearrange("p a b c d -> p (a b c d)"), da_h[:, :])
            late["pg"] = dma_load(postg[:], postg_h[:, :])

        def load_late2():
            late["pg32"] = S.emit("dve", lambda h: h.tensor_scalar(out=postg[:], in0=postg[:], scalar1=32.0, scalar2=None, op0=ALU.mult),
                                  waits=[late["pg"]], sem=True)
        e_bs = dma_load(cst[:].rearrange("p a b -> p (a b)"), bs_h[:, :])
        e_csc = dma_load(gv[0][:, 0:384], csc_h[:, :])
        e_wf = dma_load(gv[1][:], wf_h[:, :])
        cc = gv[0][:, 0:128]
        scm = gv[0][:, 128:256]
        ones = gv[0][:, 256:384]

        e_wcol = {}
        e_wk = {}
        sgf = sgbT[:].rearrange("p a b -> p (a b)").bitcast(F32)
        stg = [sgf[:, i * 512:(i + 1) * 512] for i in range(8)] + [guf[:, i * 512:(i + 1) * 512] for i in range(1, 8)]
        st_ring = Ring(15)
        st_sems = [f"di{i}" for i in range(15)]
        stg_last = []
        win_pending = {}

        def win_load(c0, tag):
            lst = []
            for k in range(8):
                s_, w_ = st_ring.acquire()
                e_ld_ = dma_load(stg[s_], win_h[k * 128:(k + 1) * 128, c0:c0 + 512], waits=[w_], sem=st_sems[s_])
                lst.append((s_, e_ld_))
            win_pending[tag] = (c0, lst)

        def win_cast(tag):
            c0, lst = win_pending.pop(tag)
            e_ = None
            for k, (s_, e_ld_) in enumerate(lst):
                e_ = S.emit("dve", lambda h, k=k, s_=s_: h.tensor_scalar(
                    out=wi[:, k, c0:c0 + 512], in0=stg[s_], scalar1=preg[:, k:k + 1], scalar2=None, op0=ALU.mult),
                    waits=[e_ld_, e_cols[0]], sem=True)
                e_wk[(tag, k)] = e_
                st_ring.release(s_, [e_])
            e_wcol[tag] = e_
            stg_last.append(e_)

        wo_state = {"last": None}

        def wout_load_k(k):
            wo_state["ld"] = dma_load(xbuf[3][:], wout_h[k * 128:(k + 1) * 128, :], waits=[wo_state["last"]], sem="dwo")

        def wout_cast_k(k):
            e_cv_ = S.emit("act", lambda h: h.activation(out=wo[:, k, :], in_=xbuf[3][:], func=AF.Copy), waits=[wo_state["ld"]], sem=True)
            wo_state["last"] = e_cv_
            e_wcol["wo"] = e_cv_

        deferred = []
        for k in range(8):
            deferred.append(lambda k=k: wout_load_k(k))
            deferred.append(None)
            deferred.append(lambda k=k: wout_cast_k(k))

        e_ld = dma_load(guf[:, 0:512], wsT_h[:, :])
        e_wsb = S.emit("act", lambda h: h.activation(out=wsTb[:].rearrange("p a b -> p (a b)"), in_=guf[:, 0:512], func=AF.Copy),
                       waits=[e_ld], sem=True)
        p0, w0 = pm_ring.acquire()
        e_mmR = S.emit("pe", lambda h, p0=p0: h.matmul(pM[p0][:], lhsT=ones, rhs=guf[:, 0:512], start=True, stop=True),
                       waits=[e_ld, e_csc, w0], sem=True)
        e_cst = None
        for hh in range(4):
            e_cst = S.emit("dve", lambda h, hh=hh, p0=p0: h.scalar_tensor_tensor(
                out=cst[:, hh, :], in0=pM[p0][:, hh * 128:(hh + 1) * 128], scalar=lnb[:, hh:hh + 1], in1=cst[:, hh, :],
                op0=ALU.mult, op1=ALU.add), waits=[e_mmR, e_bs, e_cols[2]], sem=True)
        pm_ring.release(p0, [e_cst])
        e_wb = []
        for ci, cm in enumerate((cc, scm)):
            p1, w1 = pm_ring.acquire()
            e_mm = None
            for g in range(4):
                e_mm = S.emit("pe", lambda h, g=g, p1=p1, cm=cm: h.matmul(
                    pM[p1][:, g * 128:(g + 1) * 128], lhsT=cm, rhs=gv[1][:, g * 128:(g + 1) * 128], start=True, stop=True),
                    waits=[e_csc, e_wf, w1], sem=True if g == 3 else None)
            pv = pM[p1][:].rearrange("p (a b) -> p a b", a=4)
            e1 = S.emit("act", lambda h, pv=pv, ci=ci: h.activation(out=WB[:, :, 2 * ci, :], in_=pv, func=AF.Copy),
                        waits=[e_mm], sem=True)
            e2 = S.emit("act", lambda h, pv=pv, ci=ci: h.activation(out=WB[:, :, 2 * ci + 1, :], in_=pv, func=AF.Copy, scale=-1.0),
                        waits=[e_mm], sem=True)
            pm_ring.release(p1, [e2])
            e_wb.append(e2)
        gv_ring.free[0] = [e_mmR] + e_wb
        gv_ring.free[1] = list(e_wb)
        e_gate_c = [e_cst, e_wsb, e_cols[1]]
        e_B_c = e_wb + [e_cols[3]]
        e_gate_c = e_gate_c + [e_mmR]

        stat_rings = {}

        def stat_col(kind, width=1, depth=8):
            if kind not in stat_rings:
                base = sum(r[1] * r[2] for r in stat_rings.values())
                stat_rings[kind] = (Ring(depth), width, depth, base)
            ring, width, depth, base = stat_rings[kind]
            s, w = ring.acquire()
            c0 = base + s * width
            assert c0 + width <= 256
            return (c0, s, w)

        def stat_release(kind, s, evs):
            stat_rings[kind][0].release(s, evs)

        n_units = NB * 16
        s1_state = {}

        def s1_load(n):
            if n >= n_units or n in s1_state:
                return
            b, i = divmod(n, 16)
            s, w = xs_ring.acquire()
            e = dma_load(xbuf[s][:], x_h[b, i * 128:(i + 1) * 128, :], waits=[w], sem=f"dxs{s}")
            s1_state[n] = (s, e)

        ht_cur = {}

        s1_mid = {}

        def s1a(n):
            b, i = divmod(n, 16)
            blk, ii = divmod(i, 4)
            gblk = b * 4 + blk
            s, e_ld = s1_state.pop(n)
            q, qw = xb_ring.acquire()
            c_ss, k_ss, w_ss = stat_col("ssx")
            c_t, k_t, w_t = stat_col("tx")
            c_r, k_r, w_r = stat_col("rx")
            e_sq = S.emit("act", lambda h: h.activation(out=xb[q][:], in_=xbuf[s][:], func=AF.Square,
                                                        accum_out=stt[:, c_ss:c_ss + 1]),
                          waits=[e_ld, qw, w_ss, e_pconst], sem=True)
            e_t = S.emit("pool", lambda h: h.tensor_tensor(out=stt[:, c_t:c_t + 1], in0=stt[:, c_ss:c_ss + 1], in1=c_e1024, op=ALU.add),
                         waits=[e_sq, w_t, e_pconst], sem=True)
            e_r = S.emit("pool", lambda h: h.tensor_tensor(out=stt[:, c_r:c_r + 1], in0=stt[:, c_t:c_t + 1], in1=c_mh, op=ALU.pow),
                         waits=[e_t, w_r], sem=True)
            e_cv = S.emit("dve", lambda h: h.tensor_scalar(out=xb[q][:], in0=xbuf[s][:], scalar1=stt[:, c_r:c_r + 1], scalar2=32.0,
                                                           op0=ALU.mult, op1=ALU.mult),
                          waits=[e_r, e_sq], sem=True)
            stat_release("ssx", k_ss, [e_t])
            stat_release("tx", k_t, [e_r])
            stat_release("rx", k_r, [e_cv])
            if s < 3:
                xs_ring.release(s, [e_cv])
            else:
                wo_state["last"] = e_cv
            s1_load(n + 3)
            s1_mid[n] = (q, e_cv)

        def s1b(n):
            q, e_cv = s1_mid.pop(n)
            t, tw = pt_ring.acquire()
            e_tr = None
            for k in range(8):
                e_tr = S.emit("pe", lambda h, k=k: h.transpose(out=pT[t][:, k, :], in_=xb[q][:, k * 128:(k + 1) * 128], identity=ident[:]),
                              waits=[e_cv, tw, e_ident], sem=True if k == 7 else None)
            xb_ring.release(q, [e_tr])
            s1_mid[("c", n)] = (t, e_tr)

        def s1c(n):
            b, i = divmod(n, 16)
            blk, ii = divmod(i, 4)
            gblk = b * 4 + blk
            if ii == 0:
                hs, hw = ht_ring.acquire()
                ht_cur[gblk] = [hs, hw, None]
            hs, hw, _ = ht_cur[gblk]
            t, e_tr = s1_mid.pop(("c", n))
            e_ev = S.emit("act", lambda h: h.activation(out=hT[hs][:, :, ii * 128:(ii + 1) * 128], in_=pT[t][:, :, :], func=AF.Copy),
                          waits=[e_tr, hw], sem=True)
            pt_ring.release(t, [e_ev])
            ht_cur[gblk][2] = e_ev

        def mm_group(pbank_ap_list, specs, waits):
            e = None
            n = len(specs)
            for j, spec in enumerate(specs):
                oap, lhsT, rhs, st, sp_ = spec[:5]
                xw = [spec[5]] if len(spec) > 5 else []
                e = S.emit("pe", lambda h, oap=oap, lhsT=lhsT, rhs=rhs, st=st, sp_=sp_: h.matmul(oap, lhsT=lhsT, rhs=rhs, start=st, stop=sp_),
                           waits=(list(waits) + xw) if j == 0 else xw, sem=True if j == n - 1 else None)
            return e

        last_s2 = {}
        scratch_done = {}
        vn_ready = {}
        gu_ready = {}
        sgb_ready = {}
        z_ready = {}

        def s2_tasks(b, blk):
            gblk = b * 4 + blk
            hs, _, e_h = ht_cur[gblk]
            H = hT[hs]
            tok = slice(blk * 512, (blk + 1) * 512)
            tasks = []

            def z_task(r):
                p, w = pm_ring.acquire()
                e_mm = mm_group(None, [(pM[p][:], H[:, k, r:512:4], wi[:, k, 1536:2048], k == 0, k == 7, e_wk[("z", k)]) for k in range(8)],
                                [e_h, w])
                e_ev = S.emit("act", lambda h: h.activation(out=z_sb[:, blk * 4 + r, :], in_=pM[p][:], func=AF.Copy),
                              waits=[e_mm, z_ready.get(("A", b - 1)), scratch_done.get((b - 1, "z", blk))], sem=True)
                pm_ring.release(p, [e_ev])
                z_ready[(b, blk, r)] = e_ev
                last_s2[gblk] = e_mm

            def v_task(j):
                c = blk * 4 + j
                p, w = pm_ring.acquire()
                e_mm = mm_group(None, [(pM[p][:], H[:, k, j * 128:(j + 1) * 128], wi[:, k, 512:1024], k == 0, k == 7, e_wk[("v", k)]) for k in range(8)],
                                [e_h, w])
                g_, gw = gv_ring.acquire()
                e_g = S.emit("act", lambda h: h.activation(out=gv[g_][:], in_=pM[p][:], func=AF.Gelu), waits=[e_mm, gw], sem=True)
                pm_ring.release(p, [e_g])
                c_bs, k_bs, w_bs = stat_col("bns", 6)
                c_mv, k_mv, w_mv = stat_col("mv", 2)
                c_t, k_t, w_t = stat_col("tv")
                c_r, k_r, w_r = stat_col("rv")
                e_bs_ = S.emit("dve", lambda h: h.bn_stats(out=stt[:, c_bs:c_bs + 6], in_=gv[g_][:]), waits=[e_g, w_bs], sem=True)
                e_ba = S.emit("dve", lambda h: h.bn_aggr(out=stt[:, c_mv:c_mv + 2], in_=stt[:, c_bs:c_bs + 6]), waits=[e_bs_, w_mv], sem=True)
                e_t = S.emit("pool", lambda h: h.tensor_tensor(out=stt[:, c_t:c_t + 1], in0=stt[:, c_mv + 1:c_mv + 2], in1=c_eps, op=ALU.add),
                             waits=[e_ba, w_t], sem=True)
                e_r = S.emit("pool", lambda h: h.tensor_tensor(out=stt[:, c_r:c_r + 1], in0=stt[:, c_t:c_t + 1], in1=c_mh, op=ALU.pow),
                             waits=[e_t, w_r], sem=True)
                e_n = S.emit("dve", lambda h: h.tensor_scalar(out=vnG[:, c, :], in0=gv[g_][:], scalar1=stt[:, c_mv:c_mv + 1],
                                                              scalar2=stt[:, c_r:c_r + 1], op0=ALU.subtract, op1=ALU.mult),
                             waits=[e_r, e_ba, vn_ready.get(("B", b - 1)), scratch_done.get((b - 1, "v", blk))], sem=True)
                gv_ring.release(g_, [e_n])
                stat_release("bns", k_bs, [e_ba])
                stat_release("mv", k_mv, [e_n])
                stat_release("tv", k_t, [e_r])
                stat_release("rv", k_r, [e_n])
                vn_ready[(b, c)] = e_n
                last_s2[gblk] = e_mm

            def u_task(e):
                p, w = pm_ring.acquire()
                e_mm = mm_group(None, [(pM[p][:], wi[:, k, e * 128:(e + 1) * 128], H[:, k, :], k == 0, k == 7, e_wk[("u", k)]) for k in range(8)],
                                [e_h, w])
                e_g = S.emit("act", lambda h: h.activation(out=guT[:, e, tok], in_=pM[p][:], func=AF.Gelu),
                             waits=[e_mm, gu_ready.get(("W", b - 1)), e_gate_c, stg_last], sem=True)
                pm_ring.release(p, [e_g])
                gu_ready[(b, blk, e, "u")] = e_g
                last_s2[gblk] = e_mm

            def ga_task(e):
                p, w = pm_ring.acquire()
                e_mm = mm_group(None, [(pM[p][:], wi[:, k, 1024 + e * 128:1024 + (e + 1) * 128], H[:, k, :], k == 0, k == 7, e_wk[("ga", k)]) for k in range(8)],
                                [e_h, w])
                a_, aw = sga_ring.acquire()
                e_s = S.emit("act", lambda h: h.activation(out=sga[a_][:], in_=pM[p][:], func=AF.Silu), waits=[e_mm, aw], sem=True)
                pm_ring.release(p, [e_s])
                e_m = S.emit("pool", lambda h: h.tensor_tensor(out=guT[:, e, tok], in0=guT[:, e, tok], in1=sga[a_][:], op=ALU.mult),
                             waits=[e_s, gu_ready[(b, blk, e, "u")]], sem=True)
                sga_ring.release(a_, [e_m])
                gu_ready[(b, blk, e)] = e_m
                last_s2[gblk] = e_mm

            def gb_task(e):
                p, w = pm_ring.acquire()
                e_mm = mm_group(None, [(pM[p][:], wi[:, k, 2048 + e * 128:2048 + (e + 1) * 128], H[:, k, :], k == 0, k == 7, e_wk[("gb", k)]) for k in range(8)],
                                [e_h, w])
                e_s = S.emit("act", lambda h: h.activation(out=sgbT[:, e, tok], in_=pM[p][:], func=AF.Silu),
                             waits=[e_mm, gu_ready.get(("W", b - 1)), stg_last], sem=True)
                pm_ring.release(p, [e_s])
                sgb_ready[(b, blk, e)] = e_s
                last_s2[gblk] = e_mm

            def gate_task(hh):
                p, w = pm_ring.acquire()
                specs = [(pM[p][:, j * 128:(j + 1) * 128], vnG[:, blk * 4 + j, hh * 128:(hh + 1) * 128], wsTb[:, hh, :], True, True)
                         for j in range(4)]
                e_mm = mm_group(None, specs, [vn_ready[(b, blk * 4 + 3)], vn_ready[(b, blk * 4 + 2)], vn_ready[(b, blk * 4 + 1)],
                                              vn_ready[(b, blk * 4)], w, e_gate_c])
                pv = pM[p][:].rearrange("p (a b) -> p a b", a=4)
                cb = cst[:, hh, :].unsqueeze(1).broadcast_to([128, 4, 128])
                t_, tw_ = gt_ring.acquire()
                gv_ = gt[t_][:].rearrange("p (a b) -> p a b", a=4)
                e_1 = S.emit("dve", lambda h: h.scalar_tensor_tensor(out=gv_, in0=pv, scalar=lng[:, hh:hh + 1], in1=cb,
                                                                    op0=ALU.mult, op1=ALU.add), waits=[e_mm, e_gate_c, tw_], sem=True)
                pm_ring.release(p, [e_1])
                e_2 = S.emit("pool", lambda h: h.tensor_tensor(out=guT[:, hh, tok], in0=gt[t_][:], in1=guT[:, hh, tok], op=ALU.mult),
                             waits=[e_1, gu_ready[(b, blk, hh)]], sem=True)
                gt_ring.release(t_, [e_2])
                gu_ready[(b, blk, hh, "a")] = e_2
                vn_ready[("G", b, blk)] = e_mm

            for r in range(4):
                tasks.append(lambda r=r: z_task(r))
            for j in range(4):
                tasks.append(lambda j=j: v_task(j))
            for e in range(4):
                tasks.append(lambda e=e: u_task(e))
            for e in range(4):
                tasks.append(lambda e=e: ga_task(e))
            for e in range(4):
                tasks.append(lambda e=e: gb_task(e))
            for hh in range(4):
                tasks.append(lambda hh=hh: gate_task(hh))
            return tasks

        def dft_tasks(b):
            tasks = []
            g_ev = {}

            def a_task(g, r):
                gbuf = g % 2
                G = lambda j: vnG[:, gbuf * 8 + j, 0:NJ]
                p0_, w0_ = pm_ring.acquire()
                p1_, w1_ = pm_ring.acquire()
                specs = []
                for t in range(4):
                    lhsT = z_sb[:, t * 4 + r, g * 128:(g + 1) * 128]
                    specs.append((pM[p0_][:, 0:NJ], lhsT, DA[:, r, t, 0, 0:NJ], t == 0, t == 3))
                    specs.append((pM[p1_][:, 0:NJ], lhsT, DA[:, r, t, 1, 0:NJ], t == 0, t == 3))
                zw = [z_ready[(b, t, r)] for t in range(4)]
                gw = [vn_ready[("G", b, blk)] for blk in ((0, 1) if g % 2 == 0 else (2, 3))]
                e_mm = mm_group(None, specs, zw + [w0_, w1_, late["da"]])
                wprev = [g_ev.get(("B", g - 2))]
                if r < 2:
                    e0 = S.emit("act", lambda h: h.activation(out=G(2 * r), in_=pM[p0_][:, 0:NJ], func=AF.Copy),
                                waits=[e_mm] + gw + wprev, sem=True)
                    e1 = S.emit("act", lambda h: h.activation(out=G(2 * r + 1), in_=pM[p1_][:, 0:NJ], func=AF.Copy),
                                waits=[e_mm] + gw + wprev, sem=True)
                    pm_ring.release(p0_, [e0])
                    pm_ring.release(p1_, [e1])
                    g_ev[("A", g, r)] = [e0, e1]
                else:
                    src = g_ev[("A", g, r - 2)]
                    evs = []
                    for cs, pb in ((0, p0_), (1, p1_)):
                        old = 2 * (r - 2) + cs
                        new_ = 4 + 2 * (r - 2) + cs
                        e_s = S.emit("dve", lambda h, pb=pb, old=old, new_=new_: h.tensor_tensor(
                            out=G(new_), in0=pM[pb][:, 0:NJ], in1=G(old), op=ALU.add), waits=[e_mm] + src + gw + wprev, sem=True)
                        e_d = S.emit("dve", lambda h, pb=pb, old=old: h.tensor_tensor(
                            out=G(old), in0=pM[pb][:, 0:NJ], in1=G(old), op=ALU.subtract), waits=[e_s], sem=True)
                        pm_ring.release(pb, [e_d])
                        evs += [e_s, e_d]
                    g_ev[("A", g, r)] = evs
                z_ready[("A", b)] = e_mm

            BSPEC = {0: ((0, 4), (0, 6), (3, 5), (3, 7)),
                     2: ((0, 4), (1, 6), (3, 5), (2, 7)),
                     1: ((1, 0), (0, 3), (2, 1), (2, 2)),
                     3: ((1, 0), (1, 3), (2, 1), (3, 2))}

            BSPEC_UP = {0: ((1, 0), (1, 3), (3, 1), (2, 2)),
                        2: ((1, 0), (0, 3), (3, 1), (3, 2)),
                        1: ((0, 4), (1, 6), (2, 5), (3, 7)),
                        3: ((0, 4), (0, 6), (2, 5), (2, 7))}

            def b_task(g, m):
                gbuf = g % 2
                p, w = pm_ring.acquire()
                specs = []
                for n, (idx, slot) in enumerate(BSPEC[m]):
                    specs.append((pM[p][:, 0:NJ], WB[:, g, idx, :], vnG[:, gbuf * 8 + slot, 0:NJ], n == 0, n == 3))
                for n, (idx, slot) in enumerate(BSPEC_UP[m]):
                    specs.append((pM[p][:, NJ:512], WB[:, g, idx, :], vnG[:, gbuf * 8 + slot, 255:0:-1], n == 0, n == 3))
                aw = []
                for r in range(4):
                    aw += g_ev[("A", g, r)]
                e_mm = mm_group(None, specs, aw + [w, e_B_c])
                ks = slice(m * 512, (m + 1) * 512)
                sw = [sgb_ready[(b, m, g)]]
                e_o = S.emit("dve", lambda h: h.scalar_tensor_tensor(out=sgbT[:, g, ks], in0=pM[p][:], scalar=bfc[:, g:g + 1],
                                                                    in1=sgbT[:, g, ks], op0=ALU.add, op1=ALU.mult),
                             waits=[e_mm] + sw + [e_B_c], sem=True)
                pm_ring.release(p, [e_o])
                g_ev[("B", g)] = e_mm
                vn_ready[("B", b)] = e_mm
                sgb_ready[(b, "b", g, m)] = e_o

            order = [("A", 0), ("A", 1), ("B", 0), ("A", 2), ("B", 1), ("A", 3), ("B", 2), ("B", 3)]
            for kind, g in order:
                for j in range(4):
                    if kind == "A":
                        tasks.append(lambda g=g, j=j: a_task(g, j))
                    else:
                        tasks.append(lambda g=g, j=j: b_task(g, j))
            return tasks

        store_evs = []

        def wout_tasks(b):
            tasks = []
            mid = {}

            def w_front(i):
                s, w = xr_ring.acquire()
                xr = xr_buf[s]
                e_ld = dma_load(xr, x_h[b, i * 128:(i + 1) * 128, :], waits=[w, vn_ready[("B", b)]], sem=f"dxr{s}")
                p0_, w0_ = pm_ring.acquire()
                p1_, w1_ = pm_ring.acquire()
                pp = (p0_, p1_)
                ts_ = slice(i * 128, (i + 1) * 128)
                specs = []
                for e in range(8):
                    lhsT = guT[:, e, ts_] if e < 4 else sgbT[:, e - 4, ts_]
                    for dh in range(2):
                        specs.append((pM[pp[dh]][:], lhsT, wo[:, e, dh * 512:(dh + 1) * 512], e == 0, e == 7))
                blk = i // 4
                aw = [gu_ready[(b, blk, hh, "a")] for hh in range(4)] + [sgb_ready[(b, "b", g, blk)] for g in range(4)]
                e_mm = mm_group(None, specs, aw + [w0_, w1_, e_wcol["wo"]])
                gu_ready[("W", b)] = e_mm
                o_, ow = ot_ring.acquire()
                ot = ot_buf[o_]
                c_ss, k_ss, w_ss = stat_col("ssy", 2)
                c_s, k_s, w_s = stat_col("sy")
                c_t, k_t, w_t = stat_col("ty")
                c_r, k_r, w_r = stat_col("ry")
                e_sq = None
                for dh in range(2):
                    cs_ = slice(dh * 512, (dh + 1) * 512)
                    j_, jw = gv_ring.acquire()
                    e_sq = S.emit("act", lambda h, dh=dh, j_=j_: h.activation(out=gv[j_][:], in_=pM[pp[dh]][:], func=AF.Square,
                                                                            accum_out=stt[:, c_ss + dh:c_ss + dh + 1]),
                                  waits=[e_mm, jw, w_ss], sem=True)
                    gv_ring.release(j_, [e_sq])
                    e_a = S.emit("dve", lambda h, dh=dh, cs_=cs_: h.tensor_tensor(out=ot[:, cs_], in0=pM[pp[dh]][:], in1=postg[:, cs_], op=ALU.mult),
                                 waits=[e_mm, e_sq, ow, z_ready[("A", b)], late["pg32"]], sem=True)
                    pm_ring.release(pp[dh], [e_sq, e_a])
                e_s = S.emit("pool", lambda h: h.tensor_tensor(out=stt[:, c_s:c_s + 1], in0=stt[:, c_ss:c_ss + 1], in1=stt[:, c_ss + 1:c_ss + 2], op=ALU.add),
                             waits=[e_sq, w_s], sem=True)
                e_t = S.emit("pool", lambda h: h.tensor_tensor(out=stt[:, c_t:c_t + 1], in0=stt[:, c_s:c_s + 1], in1=c_e1024, op=ALU.add),
                             waits=[e_s, w_t], sem=True)
                e_r = S.emit("pool", lambda h: h.tensor_tensor(out=stt[:, c_r:c_r + 1], in0=stt[:, c_t:c_t + 1], in1=c_mh, op=ALU.pow),
                             waits=[e_t, w_r], sem=True)
                stat_release("ssy", k_ss, [e_s])
                stat_release("sy", k_s, [e_t])
                stat_release("ty", k_t, [e_r])
                mid[i] = (s, xr, e_ld, o_, ot, e_a, e_r, c_r, k_r)

            def w_back(i):
                s, xr, e_ld, o_, ot, e_a, e_r, c_r, k_r = mid.pop(i)
                e_fin = S.emit("dve", lambda h: h.scalar_tensor_tensor(out=xr, in0=ot, scalar=stt[:, c_r:c_r + 1], in1=xr,
                                                                      op0=ALU.mult, op1=ALU.add),
                               waits=[e_a, e_r, e_ld], sem=True)
                stat_release("ry", k_r, [e_fin])
                ot_ring.release(o_, [e_fin])
                e_st = S.emit("pool", lambda h: h.dma_start(out=out_h[b, i * 128:(i + 1) * 128, :], in_=xr),
                              waits=[e_fin], sem=f"dst{s}", amt=16)
                xr_ring.release(s, [e_st])
                store_evs.append(e_st)
                scratch_done[(b, "z", o_)] = e_fin
                scratch_done[(b, "v", s)] = e_st

            tasks.append(lambda: w_front(0))
            for i in range(1, 16):
                tasks.append(lambda i=i: (w_front(i), w_back(i - 1)))
            tasks.append(lambda: w_back(15))
            return tasks

        for n_ in range(3):
            s1_load(n_)
        s1_state[3] = (3, dma_load(xbuf[3][:], x_h[0, 3 * 128:4 * 128, :], sem="dxs3"))
        win_load(1536, "z")
        win_cast("z")
        win_load(512, "v")
        for st_, n_ in (("a", 0), ("a", 1), ("b", 0), ("a", 2), ("b", 1), ("c", 0), ("a", 3), ("b", 2), ("c", 1), ("b", 3), ("c", 2), ("c", 3)):
            {"a": s1a, "b": s1b, "c": s1c}[st_](n_)
        win_cast("v")
        sched = {0: [("a", 0)], 1: [("a", 1)], 5: [("b", 0), ("a", 2)], 6: [("b", 1)], 7: [("c", 0), ("a", 3)],
                 9: [("c", 1)], 10: [("b", 2)], 13: [("b", 3), ("c", 2)], 16: [("c", 3)]}
        STG = {"a": s1a, "b": s1b, "c": s1c}

        def pop_deferred():
            if deferred:
                f_ = deferred.pop(0)
                if f_ is not None:
                    f_()

        for b in range(NB):
            blk0 = 0
            if b == 0 and NB * 4 >= 3:
                t0 = s2_tasks(0, 0)
                sch1 = {0: [("a", 4)], 1: [("a", 5)], 2: [("b", 4)], 3: [("a", 6), ("b", 5)], 4: [("c", 4)],
                        5: [("a", 7), ("b", 6)], 6: [("c", 5)], 7: [("b", 7), ("c", 6), ("c", 7)]}
                for ti in range(8):
                    t0[ti]()
                    for st_, n_ in sch1.get(ti, ()):
                        STG[st_](n_)
                for c0_, tg_ in ((0, "u"), (1024, "ga"), (2048, "gb")):
                    win_load(c0_, tg_)
                    win_cast(tg_)
                t1 = s2_tasks(0, 1)
                for ti in range(8):
                    t1[ti]()
                merged = t0[8:] + t1[8:]
                sch2 = {2: [("a", 8)], 4: [("a", 9)], 8: [("b", 8)], 9: [("a", 10)], 10: [("b", 9)], 12: [("c", 8)],
                        13: [("a", 11)], 15: [("c", 9)], 16: [("b", 10)], 19: [("b", 11)], 20: [("c", 10)], 23: [("c", 11)]}
                for mi, t in enumerate(merged):
                    t()
                    if mi == 11:
                        ht_ring.release(ht_cur[0][0], [last_s2[0]])
                    if mi == 15:
                        load_late()
                    if 8 * 4 < n_units or True:
                        for st_, n_ in sch2.get(mi, ()):
                            if n_ < n_units:
                                STG[st_](n_)
                    if mi >= 16 and mi % 2 == 1:
                        pop_deferred()
                ht_ring.release(ht_cur[1][0], [last_s2[1]])
                load_late2()
                blk0 = 2
            dts = dft_tasks(b)
            for blk in range(blk0, 4):
                gblk = b * 4 + blk
                tasks = s2_tasks(b, blk)
                if blk == 3:
                    tasks = tasks[:20] + dts[:2] + tasks[20:]
                    dts = dts[2:]
                nxt = (gblk + 1) * 4
                for ti, t in enumerate(tasks):
                    t()
                    if nxt < n_units:
                        for st_, j_ in sched.get(ti, ()):
                            STG[st_](nxt + j_)
                    if b == 0 and ti % 2 == 1:
                        pop_deferred()
                hs = ht_cur[gblk][0]
                ht_ring.release(hs, [last_s2[gblk]])
            while deferred:
                f_ = deferred.pop(0)
                if f_ is not None:
                    f_()
            for t in dts:
                t()
            for t in wout_tasks(b):
                t()
        S.emit("pool", lambda h: h.nop(), waits=store_evs[-6:])
        if debug:
            allev = [(n_, S.cnt[n_]) for n_ in ("pe", "act", "dve", "pool")] + store_evs[-6:]
            dbg = [("d_wi", wi[:].rearrange("p a b -> p (a b)"), BF16), ("d_wo", wo[:].rearrange("p a b -> p (a b)"), BF16),
                   ("d_WB", WB[:].rearrange("p a b c -> p (a b c)"), BF16), ("d_cst", cst[:].rearrange("p a b -> p (a b)"), F32),
                   ("d_hT0", hT[0][:].rearrange("p a b -> p (a b)"), BF16), ("d_hT1", hT[1][:].rearrange("p a b -> p (a b)"), BF16),
                   ("d_z", z_sb[:].rearrange("p a b -> p (a b)"), BF16), ("d_vnG", vnG[:].rearrange("p a b -> p (a b)"), BF16),
                   ("d_guT", guT[:].rearrange("p a b -> p (a b)"), BF16), ("d_sgbT", sgbT[:].rearrange("p a b -> p (a b)"), BF16),
                   ("d_stt", stt[:], F32), ("d_wsTb", wsTb[:].rearrange("p a b -> p (a b)"), BF16), ("d_postg", postg[:], F32)]
            evs = []
            for (nm, ap_, dt_) in dbg:
                dh_ = nc.dram_tensor(nm, [128, ap_.shape[1]], dt_, kind="ExternalOutput").ap()
                evs.append(S.emit("sp", lambda h, dh_=dh_, ap_=ap_: h.dma_start(out=dh_[:, :], in_=ap_), waits=allev, sem="ddbg", amt=16))
            S.emit("sp", lambda h: h.nop(), waits=[evs[-1]])

        for sname in sorted(S.cnt.keys()):
            S.semh[sname] = es.enter_context(nc.semaphore(sname))
        block = es.enter_context(nc.Block())

        @block.sync
        def _(h):
            S.replay("sp", h)

        @block.scalar
        def _(h):
            S.replay("act", h)

        @block.vector
        def _(h):
            S.replay("dve", h)

        @block.tensor
        def _(h):
            S.replay("pe", h)

        @block.gpsimd
        def _(h):
            S.replay("pool", h)
    return nc


def host_constants():
    j = np.arange(NJP, dtype=np.int64)
    p = np.arange(128, dtype=np.int64)
    da = np.zeros((128, 4, 4, 2, NJP), dtype=np.float64)
    for r in range(4):
        for t in range(4):
            s = 4 * (128 * t + p) + r
            ang = 2.0 * np.pi * ((s[:, None] * j[None, :]) % 2048) / 2048.0
            da[:, r, t, 0, :] = np.cos(ang) / 512.0
            da[:, r, t, 1, :] = np.sin(ang) / 512.0
    c = np.arange(128, dtype=np.int64)
    angc = 2.0 * np.pi * ((c[:, None] * c[None, :]) % 128) / 128.0
    csc = np.concatenate([np.cos(angc), np.sin(angc), np.ones((128, 128))], axis=1).astype(np.float32)
    ident = np.eye(128, dtype=np.float32).astype(ml_dtypes.bfloat16)
    return (np.ascontiguousarray(da.reshape(128, -1).astype(np.float32).astype(ml_dtypes.bfloat16)),
            np.ascontiguousarray(csc), ident)


def make_in_maps(x, pre_g, post_g, w_in, ln_g, ln_b, w_s, b_s, w_f, b_f, w_out, n_cores, nb):
    f = np.float32
    da, csc, ident = host_constants()
    shared = {
        "w_in": np.ascontiguousarray(w_in[0], dtype=f),
        "w_out": np.ascontiguousarray(w_out[0], dtype=f),
        "preg_col": np.ascontiguousarray(pre_g[0].reshape(8, 128).T, dtype=f),
        "postg_bc": np.ascontiguousarray(np.broadcast_to(post_g[0][None, :], (128, D)), dtype=f),
        "lng_col": np.ascontiguousarray(ln_g[0].reshape(4, 128).T, dtype=f),
        "lnb_col": np.ascontiguousarray(ln_b[0].reshape(4, 128).T, dtype=f),
        "wsT": np.ascontiguousarray(w_s[0].transpose(2, 0, 1).reshape(128, 512), dtype=f),
        "bs_bc": np.ascontiguousarray(np.broadcast_to(b_s[0].reshape(1, 512), (128, 512)), dtype=f),
        "wf": np.ascontiguousarray(w_f[0].transpose(1, 0, 2).reshape(128, 512), dtype=f),
        "bf_col": np.ascontiguousarray(b_f[0].T, dtype=f),
        "csc": csc,
        "ident": ident,
        "da": da,
    }
    maps = []
    for c in range(n_cores):
        m = dict(shared)
        m["x"] = np.ascontiguousarray(x[c * nb:(c + 1) * nb], dtype=f)
        maps.append(m)
    return maps


_PROG = {}


def kernel(x, pre_g, post_g, w_in, ln_g, ln_b, w_s, b_s, w_f, b_f, w_out):
    args = [np.asarray(a) for a in (x, pre_g, post_g, w_in, ln_g, ln_b, w_s, b_s, w_f, b_f, w_out)]
    nb = B_TOTAL // N_CORES
    if nb not in _PROG:
        _PROG[nb] = build_program(nb)
    nc = _PROG[nb]
    in_maps = make_in_maps(*args, n_cores=N_CORES, nb=nb)
    res = run_bass_kernel_spmd(nc, in_maps, core_ids=list(range(N_CORES)))
    out = np.concatenate([np.asarray(r["out"]) for r in res.results], axis=0)
    return out.astype(np.float32, copy=False)
```

```python
import numpy as np
import ml_dtypes
from contextlib import ExitStack

import concourse.bass as bass
import concourse.mybir as mybir
from concourse.bass_utils import run_bass_kernel_spmd

F32 = mybir.dt.float32
BF16 = mybir.dt.bfloat16
AF = mybir.ActivationFunctionType
ALU = mybir.AluOpType

N_CORES = 8
B_TOTAL = 32
SEQ = 2048
D = 1024
D_IN = 2560
EPS = 1e-6
NJ = 257
NJP = 258
ENGS = ("pe", "act", "dve", "pool", "sp")


class Sched:
    def __init__(self):
        self.ops = {e: [] for e in ENGS}
        self.cnt = {}
        self.semh = {}

    def emit(self, eng, fn, waits=(), sem=None, amt=None):
        ev = None
        if sem is True:
            sem = eng
        if sem is not None:
            if amt is None:
                amt = 1
            self.cnt[sem] = self.cnt.get(sem, 0) + amt
            ev = (sem, self.cnt[sem])
        ws = []
        for w in waits:
            if w is None:
                continue
            if isinstance(w, list):
                ws.extend([q for q in w if q is not None])
            else:
                ws.append(w)
        self.ops[eng].append((fn, ws, (sem, amt) if sem is not None else None))
        return ev

    def replay(self, eng, h):
        seen = {}
        for fn, ws, inc in self.ops[eng]:
            need = {}
            for (s, v) in ws:
                if seen.get(s, 0) >= v:
                    continue
                need[s] = max(need.get(s, 0), v)
            items = list(need.items())
            for (s, v) in items[:-1]:
                h.wait_ge(self.semh[s], v)
                seen[s] = v
            ins = fn(h)
            if items:
                s, v = items[-1]
                ins._wait_ge(self.semh[s], v)
                seen[s] = v
            if inc is not None:
                ins.then_inc(self.semh[inc[0]], inc[1])


class Ring:
    def __init__(self, n):
        self.n = n
        self.i = 0
        self.free = [[] for _ in range(n)]

    def acquire(self):
        s = self.i % self.n
        self.i += 1
        w = self.free[s]
        self.free[s] = []
        return s, w

    def release(self, s, evs):
        self.free[s] = [e for e in evs if e is not None]


def build_program(NB, debug=False):
    nc = bass.Bass("TRN2", target_bir_lowering=False)

    def din(name, shape, dt=F32):
        return nc.dram_tensor(name, list(shape), dt, kind="ExternalInput").ap()

    x_h = din("x", [NB, SEQ, D])
    win_h = din("w_in", [D, D_IN])
    wout_h = din("w_out", [D, D])
    preg_h = din("preg_col", [128, 8])
    postg_h = din("postg_bc", [128, D])
    lng_h = din("lng_col", [128, 4])
    lnb_h = din("lnb_col", [128, 4])
    wsT_h = din("wsT", [128, 512])
    bs_h = din("bs_bc", [128, 512])
    wf_h = din("wf", [128, 512])
    bf_h = din("bf_col", [128, 4])
    csc_h = din("csc", [128, 384])
    ident_h = din("ident", [128, 128], BF16)
    da_h = din("da", [128, 4 * 4 * 2 * NJP], BF16)
    out_h = nc.dram_tensor("out", [NB, SEQ, D], F32, kind="ExternalOutput").ap()

    S = Sched()
    es = ExitStack()
    with es:
        def sb(name, shape, dt):
            return es.enter_context(nc.sbuf_tensor("s_" + name, list(shape), dt))

        def ps(name, shape, dt):
            return es.enter_context(nc.psum_tensor("p_" + name, list(shape), dt))

        wi = sb("wi", [128, 8, D_IN], BF16)
        wo = sb("wo", [128, 8, D], BF16)
        DA = sb("DA", [128, 4, 4, 2, NJP], BF16)
        WB = sb("WB", [128, 4, 4, 128], BF16)
        wsTb = sb("wsTb", [128, 4, 128], BF16)
        cst = sb("cst", [128, 4, 128], F32)
        postg = sb("postg", [128, D], F32)
        ident = sb("ident", [128, 128], BF16)
        cols = sb("cols", [128, 32], F32)
        stt = sb("stt", [128, 256], F32)
        hT = [sb(f"hT{i}", [128, 8, 512], BF16) for i in range(2)]
        z_sb = sb("z_sb", [128, 16, 512], BF16)
        vnG = sb("vnG", [128, 16, 512], BF16)
        guT = sb("guT", [128, 4, SEQ], BF16)
        sgbT = sb("sgbT", [128, 4, SEQ], BF16)
        xbuf = [sb(f"xbuf{i}", [128, D], F32) for i in range(4)]
        xb = [sb(f"xb{i}", [128, D], BF16) for i in range(2)]
        gv = [sb(f"gv{i}", [128, 512], F32) for i in range(4)]
        sga = [sb(f"sga{i}", [128, 512], BF16) for i in range(4)]
        gt = [sb(f"gt{i}", [128, 512], F32) for i in range(4)]
        pT = [ps(f"pT{i}", [128, 8, 128], BF16) for i in range(2)]
        pM = [ps(f"pM{i}", [128, 512], F32) for i in range(6)]

        zf = z_sb[:].rearrange("p a b -> p (a b)").bitcast(F32)
        vf = vnG[:].rearrange("p a b -> p (a b)").bitcast(F32)
        guf = guT[:].rearrange("p a b -> p (a b)").bitcast(F32)
        ot_buf = [zf[:, i * 1024:(i + 1) * 1024] for i in range(4)]
        hTf = [hT[i][:].rearrange("p a b -> p (a b)").bitcast(F32) for i in range(2)]
        xl_buf = [hTf[0][:, 0:1024], hTf[0][:, 1024:2048], hTf[1][:, 0:1024], hTf[1][:, 1024:2048]]
        xl_pre = {}
        xr_buf = [vf[:, i * 1024:(i + 1) * 1024] for i in range(4)]
        preg = cols[:, 0:8]
        lng = cols[:, 8:12]
        lnb = cols[:, 12:16]
        bfc = cols[:, 16:20]
        c_e1024 = cols[:, 20:21]
        c_eps = cols[:, 21:22]
        c_mh = cols[:, 22:23]

        pm_ring = Ring(6)
        pt_ring = Ring(2)
        ht_ring = Ring(2)
        xs_ring = Ring(3)
        xr_ring = Ring(4)
        ot_ring = Ring(4)
        xb_ring = Ring(2)
        gv_ring = Ring(4)
        sga_ring = Ring(4)
        gt_ring = Ring(4)
        dma_n = [0]

        def dma_load(dst, src, waits=(), sem=None):
            if sem is None:
                sem = f"dl{dma_n[0]}"
                dma_n[0] += 1
            return S.emit("sp", lambda h: h.dma_start(out=dst, in_=src), waits=waits, sem=sem, amt=16)

        e_cols = [
            dma_load(preg, preg_h[:, :]),
            dma_load(lng, lng_h[:, :]),
            dma_load(lnb, lnb_h[:, :]),
            dma_load(bfc, bf_h[:, :]),
        ]
        e_c1 = S.emit("pool", lambda h: h.memset(c_e1024, 1024.0 * EPS), sem=True)
        e_c2 = S.emit("pool", lambda h: h.memset(c_eps, EPS), sem=True)
        e_c3 = S.emit("pool", lambda h: h.memset(c_mh, -0.5), sem=True)
        e_stz = S.emit("pool", lambda h: h.memset(stt[:], 0.0), sem=True)
        e_pconst = [e_c1, e_c2, e_c3, e_stz]
        e_ident = dma_load(ident[:], ident_h[:, :])
        late = {}

        def load_late():
            late["da"] = dma_load(DA[:].rearrange("p a b c d -> p (a b c d)"), da_h[:, :])
            late["pg"] = dma_load(postg[:], postg_h[:, :])

        def load_late2():
            late["pg32"] = S.emit("dve", lambda h: h.tensor_scalar(out=postg[:], in0=postg[:], scalar1=32.0, scalar2=None, op0=ALU.mult),
                                  waits=[late["pg"]], sem=True)
        e_bs = dma_load(cst[:].rearrange("p a b -> p (a b)"), bs_h[:, :])
        e_csc = dma_load(gv[0][:, 0:384], csc_h[:, :])
        e_wf = dma_load(gv[1][:], wf_h[:, :])
        cc = gv[0][:, 0:128]
        scm = gv[0][:, 128:256]
        ones = gv[0][:, 256:384]

        e_wcol = {}
        e_wk = {}
        sgf = sgbT[:].rearrange("p a b -> p (a b)").bitcast(F32)
        stg = [sgf[:, i * 512:(i + 1) * 512] for i in range(8)] + [guf[:, i * 512:(i + 1) * 512] for i in range(1, 8)]
        st_ring = Ring(15)
        st_sems = [f"di{i}" for i in range(15)]
        stg_last = []
        win_pending = {}

        def win_load(c0, tag):
            lst = []
            for k in range(8):
                s_, w_ = st_ring.acquire()
                e_ld_ = dma_load(stg[s_], win_h[k * 128:(k + 1) * 128, c0:c0 + 512], waits=[w_], sem=st_sems[s_])
                lst.append((s_, e_ld_))
            win_pending[tag] = (c0, lst)

        def win_cast(tag):
            c0, lst = win_pending.pop(tag)
            e_ = None
            for k, (s_, e_ld_) in enumerate(lst):
                e_ = S.emit("dve", lambda h, k=k, s_=s_: h.tensor_scalar(
                    out=wi[:, k, c0:c0 + 512], in0=stg[s_], scalar1=preg[:, k:k + 1], scalar2=None, op0=ALU.mult),
                    waits=[e_ld_, e_cols[0]], sem=True)
                e_wk[(tag, k)] = e_
                st_ring.release(s_, [e_])
            e_wcol[tag] = e_
            stg_last.append(e_)

        wo_state = {"last": None}

        def wout_load_k(k):
            wo_state["ld"] = dma_load(xbuf[3][:], wout_h[k * 128:(k + 1) * 128, :], waits=[wo_state["last"]], sem="dwo")

        def wout_cast_k(k):
            e_cv_ = S.emit("act", lambda h: h.activation(out=wo[:, k, :], in_=xbuf[3][:], func=AF.Copy), waits=[wo_state["ld"]], sem=True)
            wo_state["last"] = e_cv_
            e_wcol["wo"] = e_cv_

        deferred = []
        for k in range(8):
            deferred.append(lambda k=k: wout_load_k(k))
            deferred.append(None)
            deferred.append(lambda k=k: wout_cast_k(k))

        e_ld = dma_load(guf[:, 0:512], wsT_h[:, :])
        e_wsb = S.emit("act", lambda h: h.activation(out=wsTb[:].rearrange("p a b -> p (a b)"), in_=guf[:, 0:512], func=AF.Copy),
                       waits=[e_ld], sem=True)
        p0, w0 = pm_ring.acquire()
        e_mmR = S.emit("pe", lambda h, p0=p0: h.matmul(pM[p0][:], lhsT=ones, rhs=guf[:, 0:512], start=True, stop=True),
                       waits=[e_ld, e_csc, w0], sem=True)
        e_cst = None
        for hh in range(4):
            e_cst = S.emit("dve", lambda h, hh=hh, p0=p0: h.scalar_tensor_tensor(
                out=cst[:, hh, :], in0=pM[p0][:, hh * 128:(hh + 1) * 128], scalar=lnb[:, hh:hh + 1], in1=cst[:, hh, :],
                op0=ALU.mult, op1=ALU.add), waits=[e_mmR, e_bs, e_cols[2]], sem=True)
        pm_ring.release(p0, [e_cst])
        e_wb = []
        for ci, cm in enumerate((cc, scm)):
            p1, w1 = pm_ring.acquire()
            e_mm = None
            for g in range(4):
                e_mm = S.emit("pe", lambda h, g=g, p1=p1, cm=cm: h.matmul(
                    pM[p1][:, g * 128:(g + 1) * 128], lhsT=cm, rhs=gv[1][:, g * 128:(g + 1) * 128], start=True, stop=True),
                    waits=[e_csc, e_wf, w1], sem=True if g == 3 else None)
            pv = pM[p1][:].rearrange("p (a b) -> p a b", a=4)
            e1 = S.emit("act", lambda h, pv=pv, ci=ci: h.activation(out=WB[:, :, 2 * ci, :], in_=pv, func=AF.Copy),
                        waits=[e_mm], sem=True)
            e2 = S.emit("act", lambda h, pv=pv, ci=ci: h.activation(out=WB[:, :, 2 * ci + 1, :], in_=pv, func=AF.Copy, scale=-1.0),
                        waits=[e_mm], sem=True)
            pm_ring.release(p1, [e2])
            e_wb.append(e2)
        gv_ring.free[0] = [e_mmR] + e_wb
        gv_ring.free[1] = list(e_wb)
        e_gate_c = [e_cst, e_wsb, e_cols[1]]
        e_B_c = e_wb + [e_cols[3]]
        e_gate_c = e_gate_c + [e_mmR]

        stat_rings = {}

        def stat_col(kind, width=1, depth=8):
            if kind not in stat_rings:
                base = sum(r[1] * r[2] for r in stat_rings.values())
                stat_rings[kind] = (Ring(depth), width, depth, base)
            ring, width, depth, base = stat_rings[kind]
            s, w = ring.acquire()
            c0 = base + s * width
            assert c0 + width <= 256
            return (c0, s, w)

        def stat_release(kind, s, evs):
            stat_rings[kind][0].release(s, evs)

        n_units = NB * 16
        s1_state = {}

        def s1_load(n):
            if n >= n_units or n in s1_state:
                return
            b, i = divmod(n, 16)
            s, w = xs_ring.acquire()
            e = dma_load(xbuf[s][:], x_h[b, i * 128:(i + 1) * 128, :], waits=[w], sem=f"dxs{s}")
            s1_state[n] = (s, e)

        ht_cur = {}

        s1_mid = {}

        def s1a(n):
            b, i = divmod(n, 16)
            blk, ii = divmod(i, 4)
            gblk = b * 4 + blk
            s, e_ld = s1_state.pop(n)
            q, qw = xb_ring.acquire()
            c_ss, k_ss, w_ss = stat_col("ssx")
            c_t, k_t, w_t = stat_col("tx")
            c_r, k_r, w_r = stat_col("rx")
            e_sq = S.emit("act", lambda h: h.activation(out=xb[q][:], in_=xbuf[s][:], func=AF.Square,
                                                        accum_out=stt[:, c_ss:c_ss + 1]),
                          waits=[e_ld, qw, w_ss, e_pconst], sem=True)
            e_t = S.emit("pool", lambda h: h.tensor_tensor(out=stt[:, c_t:c_t + 1], in0=stt[:, c_ss:c_ss + 1], in1=c_e1024, op=ALU.add),
                         waits=[e_sq, w_t, e_pconst], sem=True)
            e_r = S.emit("pool", lambda h: h.tensor_tensor(out=stt[:, c_r:c_r + 1], in0=stt[:, c_t:c_t + 1], in1=c_mh, op=ALU.pow),
                         waits=[e_t, w_r], sem=True)
            e_cv = S.emit("dve", lambda h: h.tensor_scalar(out=xb[q][:], in0=xbuf[s][:], scalar1=stt[:, c_r:c_r + 1], scalar2=32.0,
                                                           op0=ALU.mult, op1=ALU.mult),
                          waits=[e_r, e_sq], sem=True)
            stat_release("ssx", k_ss, [e_t])
            stat_release("tx", k_t, [e_r])
            stat_release("rx", k_r, [e_cv])
            if s < 3:
                xs_ring.release(s, [e_cv])
            else:
                wo_state["last"] = e_cv
            s1_load(n + 3)
            s1_mid[n] = (q, e_cv)

        def s1b(n):
            q, e_cv = s1_mid.pop(n)
            t, tw = pt_ring.acquire()
            e_tr = None
            for k in range(8):
                e_tr = S.emit("pe", lambda h, k=k: h.transpose(out=pT[t][:, k, :], in_=xb[q][:, k * 128:(k + 1) * 128], identity=ident[:]),
                              waits=[e_cv, tw, e_ident], sem=True if k == 7 else None)
            xb_ring.release(q, [e_tr])
            s1_mid[("c", n)] = (t, e_tr)

        def s1c(n):
            b, i = divmod(n, 16)
            blk, ii = divmod(i, 4)
            gblk = b * 4 + blk
            if ii == 0:
                hs, hw = ht_ring.acquire()
                ht_cur[gblk] = [hs, hw, None]
            hs, hw, _ = ht_cur[gblk]
            t, e_tr = s1_mid.pop(("c", n))
            e_ev = S.emit("act", lambda h: h.activation(out=hT[hs][:, :, ii * 128:(ii + 1) * 128], in_=pT[t][:, :, :], func=AF.Copy),
                          waits=[e_tr, hw], sem=True)
            pt_ring.release(t, [e_ev])
            ht_cur[gblk][2] = e_ev

        def mm_group(pbank_ap_list, specs, waits):
            e = None
            n = len(specs)
            for j, spec in enumerate(specs):
                oap, lhsT, rhs, st, sp_ = spec[:5]
                xw = [spec[5]] if len(spec) > 5 else []
                e = S.emit("pe", lambda h, oap=oap, lhsT=lhsT, rhs=rhs, st=st, sp_=sp_: h.matmul(oap, lhsT=lhsT, rhs=rhs, start=st, stop=sp_),
                           waits=(list(waits) + xw) if j == 0 else xw, sem=True if j == n - 1 else None)
            return e

        last_s2 = {}
        scratch_done = {}
        vn_ready = {}
        gu_ready = {}
        sgb_ready = {}
        z_ready = {}

        def s2_tasks(b, blk):
            gblk = b * 4 + blk
            hs, _, e_h = ht_cur[gblk]
            H = hT[hs]
            tok = slice(blk * 512, (blk + 1) * 512)
            tasks = []

            def z_task(r):
                p, w = pm_ring.acquire()
                e_mm = mm_group(None, [(pM[p][:], H[:, k, r:512:4], wi[:, k, 1536:2048], k == 0, k == 7, e_wk[("z", k)]) for k in range(8)],
                                [e_h, w])
                e_ev = S.emit("act", lambda h: h.activation(out=z_sb[:, blk * 4 + r, :], in_=pM[p][:], func=AF.Copy),
                              waits=[e_mm, z_ready.get(("A", b - 1)), scratch_done.get((b - 1, "z", blk))], sem=True)
                pm_ring.release(p, [e_ev])
                z_ready[(b, blk, r)] = e_ev
                last_s2[gblk] = e_mm

            def v_task(j):
                c = blk * 4 + j
                p, w = pm_ring.acquire()
                e_mm = mm_group(None, [(pM[p][:], H[:, k, j * 128:(j + 1) * 128], wi[:, k, 512:1024], k == 0, k == 7, e_wk[("v", k)]) for k in range(8)],
                                [e_h, w])
                g_, gw = gv_ring.acquire()
                e_g = S.emit("act", lambda h: h.activation(out=gv[g_][:], in_=pM[p][:], func=AF.Gelu), waits=[e_mm, gw], sem=True)
                pm_ring.release(p, [e_g])
                c_bs, k_bs, w_bs = stat_col("bns", 6)
                c_mv, k_mv, w_mv = stat_col("mv", 2)
                c_t, k_t, w_t = stat_col("tv")
                c_r, k_r, w_r = stat_col("rv")
                e_bs_ = S.emit("dve", lambda h: h.bn_stats(out=stt[:, c_bs:c_bs + 6], in_=gv[g_][:]), waits=[e_g, w_bs], sem=True)
                e_ba = S.emit("dve", lambda h: h.bn_aggr(out=stt[:, c_mv:c_mv + 2], in_=stt[:, c_bs:c_bs + 6]), waits=[e_bs_, w_mv], sem=True)
                e_t = S.emit("pool", lambda h: h.tensor_tensor(out=stt[:, c_t:c_t + 1], in0=stt[:, c_mv + 1:c_mv + 2], in1=c_eps, op=ALU.add),
                             waits=[e_ba, w_t], sem=True)
                e_r = S.emit("pool", lambda h: h.tensor_tensor(out=stt[:, c_r:c_r + 1], in0=stt[:, c_t:c_t + 1], in1=c_mh, op=ALU.pow),
                             waits=[e_t, w_r], sem=True)
                e_n = S.emit("dve", lambda h: h.tensor_scalar(out=vnG[:, c, :], in0=gv[g_][:], scalar1=stt[:, c_mv:c_mv + 1],
                                                              scalar2=stt[:, c_r:c_r + 1], op0=ALU.subtract, op1=ALU.mult),
                             waits=[e_r, e_ba, vn_ready.get(("B", b - 1)), scratch_done.get((b - 1, "v", blk))], sem=True)
                gv_ring.release(g_, [e_n])
                stat_release("bns", k_bs, [e_ba])
                stat_release("mv", k_mv, [e_n])
                stat_release("tv", k_t, [e_r])
                stat_release("rv", k_r, [e_n])
                vn_ready[(b, c)] = e_n
                last_s2[gblk] = e_mm

            def u_task(e):
                p, w = pm_ring.acquire()
                e_mm = mm_group(None, [(pM[p][:], wi[:, k, e * 128:(e + 1) * 128], H[:, k, :], k == 0, k == 7, e_wk[("u", k)]) for k in range(8)],
                                [e_h, w])
                e_g = S.emit("act", lambda h: h.activation(out=guT[:, e, tok], in_=pM[p][:], func=AF.Gelu),
                             waits=[e_mm, gu_ready.get(("W", b - 1)), e_gate_c, stg_last], sem=True)
                pm_ring.release(p, [e_g])
                gu_ready[(b, blk, e, "u")] = e_g
                last_s2[gblk] = e_mm

            def ga_task(e):
                p, w = pm_ring.acquire()
                e_mm = mm_group(None, [(pM[p][:], wi[:, k, 1024 + e * 128:1024 + (e + 1) * 128], H[:, k, :], k == 0, k == 7, e_wk[("ga", k)]) for k in range(8)],
                                [e_h, w])
                a_, aw = sga_ring.acquire()
                e_s = S.emit("act", lambda h: h.activation(out=sga[a_][:], in_=pM[p][:], func=AF.Silu), waits=[e_mm, aw], sem=True)
                pm_ring.release(p, [e_s])
                e_m = S.emit("pool", lambda h: h.tensor_tensor(out=guT[:, e, tok], in0=guT[:, e, tok], in1=sga[a_][:], op=ALU.mult),
                             waits=[e_s, gu_ready[(b, blk, e, "u")]], sem=True)
                sga_ring.release(a_, [e_m])
                gu_ready[(b, blk, e)] = e_m
                last_s2[gblk] = e_mm

            def gb_task(e):
                p, w = pm_ring.acquire()
                e_mm = mm_group(None, [(pM[p][:], wi[:, k, 2048 + e * 128:2048 + (e + 1) * 128], H[:, k, :], k == 0, k == 7, e_wk[("gb", k)]) for k in range(8)],
                                [e_h, w])
                e_s = S.emit("act", lambda h: h.activation(out=sgbT[:, e, tok], in_=pM[p][:], func=AF.Silu),
                             waits=[e_mm, gu_ready.get(("W", b - 1)), stg_last], sem=True)
                pm_ring.release(p, [e_s])
                sgb_ready[(b, blk, e)] = e_s
                last_s2[gblk] = e_mm

            def gate_task(hh):
                p, w = pm_ring.acquire()
                specs = [(pM[p][:, j * 128:(j + 1) * 128], vnG[:, blk * 4 + j, hh * 128:(hh + 1) * 128], wsTb[:, hh, :], True, True)
                         for j in range(4)]
                e_mm = mm_group(None, specs, [vn_ready[(b, blk * 4 + 3)], vn_ready[(b, blk * 4 + 2)], vn_ready[(b, blk * 4 + 1)],
                                              vn_ready[(b, blk * 4)], w, e_gate_c])
                pv = pM[p][:].rearrange("p (a b) -> p a b", a=4)
                cb = cst[:, hh, :].unsqueeze(1).broadcast_to([128, 4, 128])
                t_, tw_ = gt_ring.acquire()
                gv_ = gt[t_][:].rearrange("p (a b) -> p a b", a=4)
                e_1 = S.emit("dve", lambda h: h.scalar_tensor_tensor(out=gv_, in0=pv, scalar=lng[:, hh:hh + 1], in1=cb,
                                                                    op0=ALU.mult, op1=ALU.add), waits=[e_mm, e_gate_c, tw_], sem=True)
                pm_ring.release(p, [e_1])
                e_2 = S.emit("pool", lambda h: h.tensor_tensor(out=guT[:, hh, tok], in0=gt[t_][:], in1=guT[:, hh, tok], op=ALU.mult),
                             waits=[e_1, gu_ready[(b, blk, hh)]], sem=True)
                gt_ring.release(t_, [e_2])
                gu_ready[(b, blk, hh, "a")] = e_2
                vn_ready[("G", b, blk)] = e_mm

            for r in range(4):
                tasks.append(lambda r=r: z_task(r))
            for j in range(4):
                tasks.append(lambda j=j: v_task(j))
            for e in range(4):
                tasks.append(lambda e=e: u_task(e))
            for e in range(4):
                tasks.append(lambda e=e: ga_task(e))
            for e in range(4):
                tasks.append(lambda e=e: gb_task(e))
            for hh in range(4):
                tasks.append(lambda hh=hh: gate_task(hh))
            return tasks

        def dft_tasks(b):
            tasks = []
            g_ev = {}

            def a_task(g, r):
                gbuf = g % 2
                G = lambda j: vnG[:, gbuf * 8 + j, 0:NJ]
                p0_, w0_ = pm_ring.acquire()
                p1_, w1_ = pm_ring.acquire()
                specs = []
                for t in range(4):
                    lhsT = z_sb[:, t * 4 + r, g * 128:(g + 1) * 128]
                    specs.append((pM[p0_][:, 0:NJ], lhsT, DA[:, r, t, 0, 0:NJ], t == 0, t == 3))
                    specs.append((pM[p1_][:, 0:NJ], lhsT, DA[:, r, t, 1, 0:NJ], t == 0, t == 3))
                zw = [z_ready[(b, t, r)] for t in range(4)]
                gw = [vn_ready[("G", b, blk)] for blk in range(4)]
                e_mm = mm_group(None, specs, zw + [w0_, w1_, late["da"]])
                wprev = [g_ev.get(("B", g - 2))]
                if r < 2:
                    e0 = S.emit("act", lambda h: h.activation(out=G(2 * r), in_=pM[p0_][:, 0:NJ], func=AF.Copy),
                                waits=[e_mm] + gw + wprev, sem=True)
                    e1 = S.emit("act", lambda h: h.activation(out=G(2 * r + 1), in_=pM[p1_][:, 0:NJ], func=AF.Copy),
                                waits=[e_mm] + gw + wprev, sem=True)
                    pm_ring.release(p0_, [e0])
                    pm_ring.release(p1_, [e1])
                    g_ev[("A", g, r)] = [e0, e1]
                else:
                    src = g_ev[("A", g, r - 2)]
                    evs = []
                    for cs, pb in ((0, p0_), (1, p1_)):
                        old = 2 * (r - 2) + cs
                        new_ = 4 + 2 * (r - 2) + cs
                        e_s = S.emit("dve", lambda h, pb=pb, old=old, new_=new_: h.tensor_tensor(
                            out=G(new_), in0=pM[pb][:, 0:NJ], in1=G(old), op=ALU.add), waits=[e_mm] + src + gw + wprev, sem=True)
                        e_d = S.emit("dve", lambda h, pb=pb, old=old: h.tensor_tensor(
                            out=G(old), in0=pM[pb][:, 0:NJ], in1=G(old), op=ALU.subtract), waits=[e_s], sem=True)
                        pm_ring.release(pb, [e_d])
                        evs += [e_s, e_d]
                    g_ev[("A", g, r)] = evs
                z_ready[("A", b)] = e_mm

            BSPEC = {0: ((0, 4), (0, 6), (3, 5), (3, 7)),
                     2: ((0, 4), (1, 6), (3, 5), (2, 7)),
                     1: ((1, 0), (0, 3), (2, 1), (2, 2)),
                     3: ((1, 0), (1, 3), (2, 1), (3, 2))}

            BSPEC_UP = {0: ((1, 0), (1, 3), (3, 1), (2, 2)),
                        2: ((1, 0), (0, 3), (3, 1), (3, 2)),
                        1: ((0, 4), (1, 6), (2, 5), (3, 7)),
                        3: ((0, 4), (0, 6), (2, 5), (2, 7))}

            def b_task(g, m):
                gbuf = g % 2
                p, w = pm_ring.acquire()
                specs = []
                for n, (idx, slot) in enumerate(BSPEC[m]):
                    specs.append((pM[p][:, 0:NJ], WB[:, g, idx, :], vnG[:, gbuf * 8 + slot, 0:NJ], n == 0, n == 3))
                for n, (idx, slot) in enumerate(BSPEC_UP[m]):
                    specs.append((pM[p][:, NJ:512], WB[:, g, idx, :], vnG[:, gbuf * 8 + slot, 255:0:-1], n == 0, n == 3))
                aw = []
                for r in range(4):
                    aw += g_ev[("A", g, r)]
                e_mm = mm_group(None, specs, aw + [w, e_B_c])
                ks = slice(m * 512, (m + 1) * 512)
                sw = [sgb_ready[(b, m, g)]]
                e_o = S.emit("dve", lambda h: h.scalar_tensor_tensor(out=sgbT[:, g, ks], in0=pM[p][:], scalar=bfc[:, g:g + 1],
                                                                    in1=sgbT[:, g, ks], op0=ALU.add, op1=ALU.mult),
                             waits=[e_mm] + sw + [e_B_c], sem=True)
                pm_ring.release(p, [e_o])
                g_ev[("B", g)] = e_mm
                vn_ready[("B", b)] = e_mm
                sgb_ready[(b, "b", g, m)] = e_o

            order = [("A", 0), ("A", 1), ("B", 0), ("A", 2), ("B", 1), ("A", 3), ("B", 2), ("B", 3)]
            for kind, g in order:
                for j in range(4):
                    if kind == "A":
                        tasks.append(lambda g=g, j=j: a_task(g, j))
                    else:
                        tasks.append(lambda g=g, j=j: b_task(g, j))
            return tasks

        store_evs = []

        def wout_tasks(b):
            tasks = []
            mid = {}

            def w_front(i):
                if (b, i) in xl_pre:
                    s = None
                    xr, e_ld = xl_pre[(b, i)]
                else:
                    s, w = xr_ring.acquire()
                    xr = xr_buf[s]
                    e_ld = dma_load(xr, x_h[b, i * 128:(i + 1) * 128, :], waits=[w, vn_ready[("B", b)]], sem=f"dxr{s}")
                p0_, w0_ = pm_ring.acquire()
                p1_, w1_ = pm_ring.acquire()
                pp = (p0_, p1_)
                ts_ = slice(i * 128, (i + 1) * 128)
                specs = []
                for e in range(8):
                    lhsT = guT[:, e, ts_] if e < 4 else sgbT[:, e - 4, ts_]
                    for dh in range(2):
                        specs.append((pM[pp[dh]][:], lhsT, wo[:, e, dh * 512:(dh + 1) * 512], e == 0, e == 7))
                blk = i // 4
                aw = [gu_ready[(b, blk, hh, "a")] for hh in range(4)] + [sgb_ready[(b, "b", g, blk)] for g in range(4)]
                e_mm = mm_group(None, specs, aw + [w0_, w1_, e_wcol["wo"]])
                gu_ready[("W", b)] = e_mm
                o_, ow = ot_ring.acquire()
                ot = ot_buf[o_]
                c_ss, k_ss, w_ss = stat_col("ssy", 2)
                c_s, k_s, w_s = stat_col("sy")
                c_t, k_t, w_t = stat_col("ty")
                c_r, k_r, w_r = stat_col("ry")
                e_sq = None
                for dh in range(2):
                    cs_ = slice(dh * 512, (dh + 1) * 512)
                    j_, jw = gv_ring.acquire()
                    e_sq = S.emit("act", lambda h, dh=dh, j_=j_: h.activation(out=gv[j_][:], in_=pM[pp[dh]][:], func=AF.Square,
                                                                            accum_out=stt[:, c_ss + dh:c_ss + dh + 1]),
                                  waits=[e_mm, jw, w_ss], sem=True)
                    gv_ring.release(j_, [e_sq])
                    e_a = S.emit("dve", lambda h, dh=dh, cs_=cs_: h.tensor_tensor(out=ot[:, cs_], in0=pM[pp[dh]][:], in1=postg[:, cs_], op=ALU.mult),
                                 waits=[e_mm, e_sq, ow, z_ready[("A", b)], late["pg32"]], sem=True)
                    pm_ring.release(pp[dh], [e_sq, e_a])
                e_s = S.emit("pool", lambda h: h.tensor_tensor(out=stt[:, c_s:c_s + 1], in0=stt[:, c_ss:c_ss + 1], in1=stt[:, c_ss + 1:c_ss + 2], op=ALU.add),
                             waits=[e_sq, w_s], sem=True)
                e_t = S.emit("pool", lambda h: h.tensor_tensor(out=stt[:, c_t:c_t + 1], in0=stt[:, c_s:c_s + 1], in1=c_e1024, op=ALU.add),
                             waits=[e_s, w_t], sem=True)
                e_r = S.emit("pool", lambda h: h.tensor_tensor(out=stt[:, c_r:c_r + 1], in0=stt[:, c_t:c_t + 1], in1=c_mh, op=ALU.pow),
                             waits=[e_t, w_r], sem=True)
                stat_release("ssy", k_ss, [e_s])
                stat_release("sy", k_s, [e_t])
                stat_release("ty", k_t, [e_r])
                mid[i] = (s, xr, e_ld, o_, ot, e_a, e_r, c_r, k_r)

            def w_back(i):
                s, xr, e_ld, o_, ot, e_a, e_r, c_r, k_r = mid.pop(i)
                e_fin = S.emit("dve", lambda h: h.scalar_tensor_tensor(out=xr, in0=ot, scalar=stt[:, c_r:c_r + 1], in1=xr,
                                                                      op0=ALU.mult, op1=ALU.add),
                               waits=[e_a, e_r, e_ld], sem=True)
                stat_release("ry", k_r, [e_fin])
                ot_ring.release(o_, [e_fin])
                e_st = S.emit("pool", lambda h: h.dma_start(out=out_h[b, i * 128:(i + 1) * 128, :], in_=xr),
                              waits=[e_fin], sem=(f"dst{s}" if s is not None else f"dsl{i}"), amt=16)
                store_evs.append(e_st)
                scratch_done[(b, "z", o_)] = e_fin
                if s is not None:
                    xr_ring.release(s, [e_st])
                    scratch_done[(b, "v", s)] = e_st

            tasks.append(lambda: w_front(0))
            for i in range(1, 16):
                tasks.append(lambda i=i: (w_front(i), w_back(i - 1)))
            tasks.append(lambda: w_back(15))
            return tasks

        for n_ in range(3):
            s1_load(n_)
        s1_state[3] = (3, dma_load(xbuf[3][:], x_h[0, 3 * 128:4 * 128, :], sem="dxs3"))
        win_load(1536, "z")
        win_cast("z")
        win_load(512, "v")
        for st_, n_ in (("a", 0), ("a", 1), ("b", 0), ("a", 2), ("b", 1), ("c", 0), ("a", 3), ("b", 2), ("c", 1), ("b", 3), ("c", 2), ("c", 3)):
            {"a": s1a, "b": s1b, "c": s1c}[st_](n_)
        win_cast("v")
        sched = {0: [("a", 0)], 1: [("a", 1)], 5: [("b", 0), ("a", 2)], 6: [("b", 1)], 7: [("c", 0), ("a", 3)],
                 9: [("c", 1)], 10: [("b", 2)], 13: [("b", 3), ("c", 2)], 16: [("c", 3)]}
        STG = {"a": s1a, "b": s1b, "c": s1c}

        def pop_deferred():
            if deferred:
                f_ = deferred.pop(0)
                if f_ is not None:
                    f_()

        for b in range(NB):
            blk0 = 0
            if b == 0 and NB * 4 >= 3:
                t0 = s2_tasks(0, 0)
                sch1 = {0: [("a", 4)], 1: [("a", 5)], 2: [("b", 4)], 3: [("a", 6), ("b", 5)], 4: [("c", 4)],
                        5: [("a", 7), ("b", 6)], 6: [("c", 5)], 7: [("b", 7), ("c", 6), ("c", 7)]}
                for ti in range(8):
                    t0[ti]()
                    for st_, n_ in sch1.get(ti, ()):
                        STG[st_](n_)
                for c0_, tg_ in ((0, "u"), (1024, "ga"), (2048, "gb")):
                    win_load(c0_, tg_)
                    win_cast(tg_)
                t1 = s2_tasks(0, 1)
                for ti in range(8):
                    t1[ti]()
                merged = t0[8:] + t1[8:]
                sch2 = {2: [("a", 8)], 4: [("a", 9)], 8: [("b", 8)], 9: [("a", 10)], 10: [("b", 9)], 12: [("c", 8)],
                        13: [("a", 11)], 15: [("c", 9)], 16: [("b", 10)], 19: [("b", 11)], 20: [("c", 10)], 23: [("c", 11)]}
                for mi, t in enumerate(merged):
                    t()
                    if mi == 11:
                        ht_ring.release(ht_cur[0][0], [last_s2[0]])
                    if mi == 15:
                        load_late()
                    if 8 * 4 < n_units or True:
                        for st_, n_ in sch2.get(mi, ()):
                            if n_ < n_units:
                                STG[st_](n_)
                    if mi >= 16 and mi % 2 == 1:
                        pop_deferred()
                ht_ring.release(ht_cur[1][0], [last_s2[1]])
                load_late2()
                blk0 = 2
            for blk in range(blk0, 4):
                gblk = b * 4 + blk
                tasks = s2_tasks(b, blk)
                nxt = (gblk + 1) * 4
                for ti, t in enumerate(tasks):
                    t()
                    if nxt < n_units:
                        for st_, j_ in sched.get(ti, ()):
                            STG[st_](nxt + j_)
                    if b == 0 and ti % 2 == 1:
                        pop_deferred()
                hs = ht_cur[gblk][0]
                ht_ring.release(hs, [last_s2[gblk]])
            while deferred:
                f_ = deferred.pop(0)
                if f_ is not None:
                    f_()
            if b == NB - 1:
                for j_, i_ in enumerate((12, 13, 14, 15)):
                    e_pl = dma_load(xl_buf[j_], x_h[b, i_ * 128:(i_ + 1) * 128, :],
                                    waits=[last_s2[4 * b + 2], last_s2[4 * b + 3]], sem=f"dxl{j_}")
                    xl_pre[(b, i_)] = (xl_buf[j_], e_pl)
            for t in dft_tasks(b):
                t()
            for t in wout_tasks(b):
                t()
        S.emit("pool", lambda h: h.nop(), waits=store_evs[-6:])
        if debug:
            allev = [(n_, S.cnt[n_]) for n_ in ("pe", "act", "dve", "pool")] + store_evs[-6:]
            dbg = [("d_wi", wi[:].rearrange("p a b -> p (a b)"), BF16), ("d_wo", wo[:].rearrange("p a b -> p (a b)"), BF16),
                   ("d_WB", WB[:].rearrange("p a b c -> p (a b c)"), BF16), ("d_cst", cst[:].rearrange("p a b -> p (a b)"), F32),
                   ("d_hT0", hT[0][:].rearrange("p a b -> p (a b)"), BF16), ("d_hT1", hT[1][:].rearrange("p a b -> p (a b)"), BF16),
                   ("d_z", z_sb[:].rearrange("p a b -> p (a b)"), BF16), ("d_vnG", vnG[:].rearrange("p a b -> p (a b)"), BF16),
                   ("d_guT", guT[:].rearrange("p a b -> p (a b)"), BF16), ("d_sgbT", sgbT[:].rearrange("p a b -> p (a b)"), BF16),
                   ("d_stt", stt[:], F32), ("d_wsTb", wsTb[:].rearrange("p a b -> p (a b)"), BF16), ("d_postg", postg[:], F32)]
            evs = []
            for (nm, ap_, dt_) in dbg:
                dh_ = nc.dram_tensor(nm, [128, ap_.shape[1]], dt_, kind="ExternalOutput").ap()
                evs.append(S.emit("sp", lambda h, dh_=dh_, ap_=ap_: h.dma_start(out=dh_[:, :], in_=ap_), waits=allev, sem="ddbg", amt=16))
            S.emit("sp", lambda h: h.nop(), waits=[evs[-1]])

        for sname in sorted(S.cnt.keys()):
            S.semh[sname] = es.enter_context(nc.semaphore(sname))
        block = es.enter_context(nc.Block())

        @block.sync
        def _(h):
            S.replay("sp", h)

        @block.scalar
        def _(h):
            S.replay("act", h)

        @block.vector
        def _(h):
            S.replay("dve", h)

        @block.tensor
        def _(h):
            S.replay("pe", h)

        @block.gpsimd
        def _(h):
            S.replay("pool", h)
    return nc


def host_constants():
    j = np.arange(NJP, dtype=np.int64)
    p = np.arange(128, dtype=np.int64)
    da = np.zeros((128, 4, 4, 2, NJP), dtype=np.float64)
    for r in range(4):
        for t in range(4):
            s = 4 * (128 * t + p) + r
            ang = 2.0 * np.pi * ((s[:, None] * j[None, :]) % 2048) / 2048.0
            da[:, r, t, 0, :] = np.cos(ang) / 512.0
            da[:, r, t, 1, :] = np.sin(ang) / 512.0
    c = np.arange(128, dtype=np.int64)
    angc = 2.0 * np.pi * ((c[:, None] * c[None, :]) % 128) / 128.0
    csc = np.concatenate([np.cos(angc), np.sin(angc), np.ones((128, 128))], axis=1).astype(np.float32)
    ident = np.eye(128, dtype=np.float32).astype(ml_dtypes.bfloat16)
    return (np.ascontiguousarray(da.reshape(128, -1).astype(np.float32).astype(ml_dtypes.bfloat16)),
            np.ascontiguousarray(csc), ident)


def make_in_maps(x, pre_g, post_g, w_in, ln_g, ln_b, w_s, b_s, w_f, b_f, w_out, n_cores, nb):
    f = np.float32
    da, csc, ident = host_constants()
    shared = {
        "w_in": np.ascontiguousarray(w_in[0], dtype=f),
        "w_out": np.ascontiguousarray(w_out[0], dtype=f),
        "preg_col": np.ascontiguousarray(pre_g[0].reshape(8, 128).T, dtype=f),
        "postg_bc": np.ascontiguousarray(np.broadcast_to(post_g[0][None, :], (128, D)), dtype=f),
        "lng_col": np.ascontiguousarray(ln_g[0].reshape(4, 128).T, dtype=f),
        "lnb_col": np.ascontiguousarray(ln_b[0].reshape(4, 128).T, dtype=f),
        "wsT": np.ascontiguousarray(w_s[0].transpose(2, 0, 1).reshape(128, 512), dtype=f),
        "bs_bc": np.ascontiguousarray(np.broadcast_to(b_s[0].reshape(1, 512), (128, 512)), dtype=f),
        "wf": np.ascontiguousarray(w_f[0].transpose(1, 0, 2).reshape(128, 512), dtype=f),
        "bf_col": np.ascontiguousarray(b_f[0].T, dtype=f),
        "csc": csc,
        "ident": ident,
        "da": da,
    }
    maps = []
    for c in range(n_cores):
        m = dict(shared)
        m["x"] = np.ascontiguousarray(x[c * nb:(c + 1) * nb], dtype=f)
        maps.append(m)
    return maps


_PROG = {}


def kernel(x, pre_g, post_g, w_in, ln_g, ln_b, w_s, b_s, w_f, b_f, w_out):
    args = [np.asarray(a) for a in (x, pre_g, post_g, w_in, ln_g, ln_b, w_s, b_s, w_f, b_f, w_out)]
    nb = B_TOTAL // N_CORES
    if nb not in _PROG:
        _PROG[nb] = build_program(nb)
    nc = _PROG[nb]
    in_maps = make_in_maps(*args, n_cores=N_CORES, nb=nb)
    res = run_bass_kernel_spmd(nc, in_maps, core_ids=list(range(N_CORES)))
    out = np.concatenate([np.asarray(r["out"]) for r in res.results], axis=0)
    return out.astype(np.float32, copy=False)
```
